# Optimizing a Trainium2 kernel written in Bass

```python
import math
import jax, jax.numpy as jnp
from jax import lax
import numpy as np

D_MODEL = 1024
BATCH = 8
SEQ = 2048
DEPTH = 1
DEC_BATCH = 128
DEC_SEQ = 4
PAST_LEN = 2048
PAGE_SIZE = 128

D_MIX = D_MODEL
HEAD_DIM = 128
GDN_HEADS = D_MIX // (2 * HEAD_DIM)
FOX_HEADS = D_MIX // (2 * HEAD_DIM)
GDN_DK = HEAD_DIM
GDN_DV = HEAD_DIM
GDN_KEY_W = GDN_HEADS * GDN_DK
GDN_WIDTH = GDN_HEADS * GDN_DV
FOX_WIDTH = FOX_HEADS * HEAD_DIM
GDN_CONV_DIM = 2 * GDN_KEY_W + GDN_WIDTH
CONV_K = 4
GDN_CHUNK = 64
Q_BLOCK = 128
NORM_EPS = 1e-6
L2_EPS = 1e-6
FGATE_BIAS_MEAN = 3.0
SPLIT_SIZES = (GDN_CONV_DIM, GDN_WIDTH, GDN_HEADS, GDN_HEADS, FOX_WIDTH, FOX_WIDTH, FOX_WIDTH, FOX_WIDTH, FOX_HEADS)
D_IN = sum(SPLIT_SIZES)

kernel_name = 'hymba_gdn_fox_decode_step'


def rms_norm(x, w):
    xf = x.astype(jnp.float32)
    y = xf * lax.rsqrt(jnp.mean(xf * xf, axis=-1, keepdims=True) + NORM_EPS)
    return (y * w.astype(jnp.float32)).astype(x.dtype)


def l2_normalize(x):
    return x * lax.rsqrt(jnp.sum(x * x, axis=-1, keepdims=True) + L2_EPS)


def split_proj(p):
    idx = [int(i) for i in np.cumsum(SPLIT_SIZES)[:-1]]
    return jnp.split(p, idx, axis=-1)


def short_conv(x, buf, w):
    L = x.shape[1]
    xx = jnp.concatenate([buf.astype(x.dtype), x], axis=1)
    y = sum(xx[:, j:j + L] * w[j] for j in range(CONV_K))
    return jax.nn.silu(y), xx[:, -(CONV_K - 1):]


def gdn_chunked(q, k, v, g, beta, s0):
    B, L, H, DK = q.shape
    DV = v.shape[-1]
    C = min(GDN_CHUNK, L)
    n = -(-L // C)
    pad = n * C - L
    if pad:
        pad4 = ((0, 0), (0, pad), (0, 0), (0, 0))
        q, k, v = jnp.pad(q, pad4), jnp.pad(k, pad4), jnp.pad(v, pad4)
        g, beta = jnp.pad(g, pad4[:3]), jnp.pad(beta, pad4[:3])
    to_c = lambda t: t.reshape(B, n, C, H, t.shape[-1]).transpose(1, 0, 3, 2, 4)
    q, k, v = to_c(q), to_c(k), to_c(v)
    g = g.reshape(B, n, C, H).transpose(1, 0, 3, 2)
    beta = beta.reshape(B, n, C, H).transpose(1, 0, 3, 2)
    gc = jnp.cumsum(g, axis=-1)
    incl = jnp.tril(jnp.ones((C, C), dtype=bool))
    strict = jnp.tril(jnp.ones((C, C), dtype=bool), -1)
    diff = gc[..., :, None] - gc[..., None, :]
    decay = jnp.where(incl, jnp.exp(jnp.where(incl, diff, 0.0)), 0.0)
    kb = k * beta[..., None]
    a = jnp.where(strict, jnp.einsum('nbhid,nbhjd->nbhij', kb, k) * decay, 0.0)
    m = a + jnp.eye(C, dtype=a.dtype)
    rhs = jnp.concatenate([v * beta[..., None], kb * jnp.exp(gc)[..., None]], axis=-1)
    sol = lax.linalg.triangular_solve(m, rhs, left_side=True, lower=True, unit_diagonal=True)
    u, w = sol[..., :DV], sol[..., DV:]
    qk = jnp.einsum('nbhid,nbhjd->nbhij', q, k) * decay
    q_dec = q * jnp.exp(gc)[..., None]
    k_dec = k * jnp.exp(gc[..., -1:] - gc)[..., None]
    g_last = jnp.exp(gc[..., -1])

    def step(s, xs):
        qk_i, qd_i, kd_i, u_i, w_i, gl_i = xs
        v_new = u_i - jnp.einsum('bhck,bhkv->bhcv', w_i, s)
        o_i = jnp.einsum('bhck,bhkv->bhcv', qd_i, s) + jnp.einsum('bhij,bhjv->bhiv', qk_i, v_new)
        s = s * gl_i[..., None, None] + jnp.einsum('bhck,bhcv->bhkv', kd_i, v_new)
        return s, o_i

    s_fin, o = lax.scan(step, s0, (qk, q_dec, k_dec, u, w, g_last))
    o = o.transpose(1, 0, 3, 2, 4).reshape(B, n * C, H, DV)[:, :L]
    return o, s_fin


def fox_attention(q, k, v, fq, fk, q_pos, k_pos):
    B, Lq, H, D = q.shape
    qb = min(Q_BLOCK, Lq)
    nb = -(-Lq // qb)
    pad = nb * qb - Lq
    if pad:
        q = jnp.pad(q, ((0, 0), (0, pad), (0, 0), (0, 0)))
        fq = jnp.pad(fq, ((0, 0), (0, pad), (0, 0)))
        q_pos = jnp.pad(q_pos, (0, pad), mode='edge')
    qs = q.reshape(B, nb, qb, H, D).swapaxes(0, 1)
    fqs = fq.reshape(B, nb, qb, H).transpose(1, 0, 3, 2)
    ps = q_pos.reshape(nb, qb)
    fk_t = fk.transpose(0, 2, 1)
    scale = D ** -0.5

    def block(xs):
        q_i, fq_i, p_i = xs
        s = jnp.einsum('bqhd,bkhd->bhqk', q_i, k, preferred_element_type=jnp.float32) * scale
        s = s + (fq_i[..., :, None] - fk_t[..., None, :])
        s = jnp.where(k_pos[None, None, None, :] <= p_i[None, None, :, None], s, -jnp.inf)
        p = jax.nn.softmax(s, axis=-1)
        return jnp.einsum('bhqk,bkhd->bqhd', p.astype(v.dtype), v)

    o = lax.map(block, (qs, fqs, ps))
    return o.swapaxes(0, 1).reshape(B, nb * qb, H, D)[:, :Lq]


def mixer_layer(x, conv_buf, ssm0, past, w_in, conv_w, a_log, dt_bias, onorm_w, f_bias, w_out, norm_w):
    B, L, _ = x.shape
    f32 = jnp.float32
    h = rms_norm(x, norm_w)
    proj = h @ w_in
    qkv_g, z_g, b_g, a_g, q_f, k_f, v_f, z_f, f_f = split_proj(proj)

    qkv_c, new_conv = short_conv(qkv_g, conv_buf, conv_w)
    q_g, k_g, v_g = jnp.split(qkv_c, [GDN_KEY_W, 2 * GDN_KEY_W], axis=-1)
    q_g = l2_normalize(q_g.reshape(B, L, GDN_HEADS, GDN_DK).astype(f32)) * (GDN_DK ** -0.5)
    k_g = l2_normalize(k_g.reshape(B, L, GDN_HEADS, GDN_DK).astype(f32))
    v_g = v_g.reshape(B, L, GDN_HEADS, GDN_DV).astype(f32)
    beta = jax.nn.sigmoid(b_g.astype(f32))
    g = -jnp.exp(a_log.astype(f32)) * jax.nn.softplus(a_g.astype(f32) + dt_bias.astype(f32))
    o_g, ssm_new = gdn_chunked(q_g, k_g, v_g, g, beta, ssm0.astype(f32))
    o_g = rms_norm(o_g, onorm_w).reshape(B, L, GDN_WIDTH) * jax.nn.silu(z_g.astype(f32))

    q_f = q_f.reshape(B, L, FOX_HEADS, HEAD_DIM)
    k_f = k_f.reshape(B, L, FOX_HEADS, HEAD_DIM)
    v_f = v_f.reshape(B, L, FOX_HEADS, HEAD_DIM)
    logf = jax.nn.log_sigmoid((f_f + f_bias).astype(f32))
    if past is None:
        fq = jnp.cumsum(logf, axis=1)
        fk, keys, vals = fq, k_f, v_f
        q_pos = jnp.arange(L)
        k_pos = q_pos
    else:
        pk, pv, plogf = past
        P = pk.shape[1]
        f_past = jnp.cumsum(plogf.astype(f32), axis=1)
        fq = f_past[:, -1:] + jnp.cumsum(logf, axis=1)
        fk = jnp.concatenate([f_past, fq], axis=1)
        keys = jnp.concatenate([pk.astype(k_f.dtype), k_f], axis=1)
        vals = jnp.concatenate([pv.astype(v_f.dtype), v_f], axis=1)
        q_pos = P + jnp.arange(L)
        k_pos = jnp.arange(P + L)
    o_f = fox_attention(q_f, keys, vals, fq, fk, q_pos, k_pos)
    o_f = o_f.reshape(B, L, FOX_WIDTH).astype(f32) * jax.nn.silu(z_f.astype(f32))

    o = jnp.concatenate([o_g, o_f], axis=-1).astype(x.dtype) @ w_out
    y = x + o
    return y, new_conv, ssm_new.astype(x.dtype), k_f, v_f, logf.astype(x.dtype)


def setup_inputs(seed: int = 0) -> dict:
    key = jax.random.key(seed)
    ks = jax.random.split(key, 20)
    nrm = jax.random.normal
    n_pages = PAST_LEN // PAGE_SIZE
    n_used = DEC_BATCH * n_pages
    n_pool = (5 * n_used + 3) // 4
    x_prompt = nrm(ks[0], (BATCH, SEQ, D_MODEL), jnp.float32)
    x_sample = nrm(ks[1], (DEC_BATCH, DEC_SEQ, D_MODEL), jnp.float32)
    cache_fox_k = nrm(ks[2], (DEPTH, n_pool, PAGE_SIZE, FOX_HEADS, HEAD_DIM), jnp.float32)
    cache_fox_v = nrm(ks[3], (DEPTH, n_pool, PAGE_SIZE, FOX_HEADS, HEAD_DIM), jnp.float32)
    cache_fox_logf = jax.nn.log_sigmoid(FGATE_BIAS_MEAN + nrm(ks[4], (DEPTH, n_pool, PAGE_SIZE, FOX_HEADS), jnp.float32))
    page_table = jax.random.permutation(ks[5], n_pool)[:n_used].reshape(DEC_BATCH, n_pages).astype(jnp.int32)
    state_gdn_ssm = 0.1 * nrm(ks[6], (DEPTH, DEC_BATCH, GDN_HEADS, GDN_DK, GDN_DV), jnp.float32)
    state_gdn_conv = nrm(ks[7], (DEPTH, DEC_BATCH, CONV_K - 1, GDN_CONV_DIM), jnp.float32)
    w_in = nrm(ks[8], (DEPTH, D_MODEL, D_IN), jnp.float32) * D_MODEL ** -0.5
    gdn_conv_w = nrm(ks[9], (DEPTH, CONV_K, GDN_CONV_DIM), jnp.float32) * CONV_K ** -0.5
    gdn_a_log = jnp.log(jax.random.uniform(ks[10], (DEPTH, GDN_HEADS), jnp.float32, 1.0, 16.0))
    dt = jnp.exp(jax.random.uniform(ks[11], (DEPTH, GDN_HEADS), jnp.float32, math.log(1e-3), math.log(1e-1)))
    gdn_dt_bias = dt + jnp.log(-jnp.expm1(-dt))
    gdn_out_norm_w = 1.0 + 0.02 * nrm(ks[12], (DEPTH, GDN_DV), jnp.float32)
    fox_f_bias = FGATE_BIAS_MEAN + 0.5 * nrm(ks[13], (DEPTH, FOX_HEADS), jnp.float32)
    w_out = nrm(ks[14], (DEPTH, D_MIX, D_MODEL), jnp.float32) * D_MIX ** -0.5
    norm_w = 1.0 + 0.02 * nrm(ks[15], (DEPTH, D_MODEL), jnp.float32)
    final_norm_w = 1.0 + 0.02 * nrm(ks[16], (D_MODEL,), jnp.float32)
    return {'x_prompt': x_prompt, 'x_sample': x_sample,
            'cache_fox_k': cache_fox_k, 'cache_fox_v': cache_fox_v, 'cache_fox_logf': cache_fox_logf,
            'page_table': page_table, 'state_gdn_ssm': state_gdn_ssm, 'state_gdn_conv': state_gdn_conv,
            'w_in': w_in, 'gdn_conv_w': gdn_conv_w, 'gdn_a_log': gdn_a_log, 'gdn_dt_bias': gdn_dt_bias,
            'gdn_out_norm_w': gdn_out_norm_w, 'fox_f_bias': fox_f_bias, 'w_out': w_out,
            'norm_w': norm_w, 'final_norm_w': final_norm_w}


def reference(x_prompt, x_sample, cache_fox_k, cache_fox_v, cache_fox_logf, page_table,
              state_gdn_ssm, state_gdn_conv, w_in, gdn_conv_w, gdn_a_log, gdn_dt_bias,
              gdn_out_norm_w, fox_f_bias, w_out, norm_w, final_norm_w):
    hp, hs = x_prompt, x_sample
    bp = x_prompt.shape[0]
    bs, n_pages = page_table.shape
    pro, sam = [], []
    for l in range(DEPTH):
        lw = (w_in[l], gdn_conv_w[l], gdn_a_log[l], gdn_dt_bias[l], gdn_out_norm_w[l],
              fox_f_bias[l], w_out[l], norm_w[l])
        conv0 = jnp.zeros((bp, CONV_K - 1, GDN_CONV_DIM), hp.dtype)
        ssm0 = jnp.zeros((bp, GDN_HEADS, GDN_DK, GDN_DV), jnp.float32)
        out_p = mixer_layer(hp, conv0, ssm0, None, *lw)
        hp = out_p[0]
        pro.append(out_p[1:])
        past = tuple(c[l][page_table].reshape(bs, n_pages * PAGE_SIZE, *c.shape[3:])
                     for c in (cache_fox_k, cache_fox_v, cache_fox_logf))
        out_s = mixer_layer(hs, state_gdn_conv[l], state_gdn_ssm[l], past, *lw)
        hs = out_s[0]
        sam.append(out_s[1:])
    y_prompt = rms_norm(hp, final_norm_w)
    y_sample = rms_norm(hs, final_norm_w)
    conv_prompt = jnp.stack([r[0] for r in pro])
    ssm_prompt = jnp.stack([r[1] for r in pro])
    k_prompt = jnp.stack([r[2] for r in pro])
    v_prompt = jnp.stack([r[3] for r in pro])
    logf_prompt = jnp.stack([r[4] for r in pro])
    conv_sample = jnp.stack([r[0] for r in sam])
    ssm_sample = jnp.stack([r[1] for r in sam])
    k_sample = jnp.stack([r[2] for r in sam])
    v_sample = jnp.stack([r[3] for r in sam])
    logf_sample = jnp.stack([r[4] for r in sam])
    return (y_prompt, y_sample, k_prompt, v_prompt, logf_prompt, ssm_prompt, conv_prompt,
            k_sample, v_sample, logf_sample, ssm_sample, conv_sample)
```

```python
import numpy as np
import concourse.bass as bass
import concourse.mybir as mybir
from concourse.bass_utils import run_bass_kernel_spmd
from contextlib import ExitStack

F32 = mybir.dt.float32
BF16 = mybir.dt.bfloat16
I32 = mybir.dt.int32
AF = mybir.ActivationFunctionType
ALU = mybir.AluOpType

T = 2048
NS = 64
D = 1024
SCALE = 128 ** -0.5
EPS = 1e-6


class Tk:
    __slots__ = ("w", "r", "excl")

    def __init__(self, excl=False):
        self.w = None
        self.r = []
        self.excl = excl


class Prog:
    ENGS = ["sp", "act", "dve", "pool", "pe"]

    def __init__(self, nc, es, ndma=10):
        self.nc = nc
        self.q = {e: [] for e in self.ENGS}
        self.cnt = {e: 0 for e in self.ENGS}
        self.seen = {e: {} for e in self.ENGS}
        self.sem = {e: es.enter_context(nc.semaphore("s_" + e)) for e in self.ENGS}
        self.semobj = {}
        for e in self.ENGS:
            self.semobj[id(self.sem[e])] = self.sem[e]
        self.dsem = {}
        self.dval = {}
        self.dpos = {}
        for e in ["sp", "pool"]:
            self.dsem[e] = [es.enter_context(nc.semaphore("d_%s%d" % (e, i))) for i in range(ndma)]
            self.dval[e] = [0] * ndma
            self.dpos[e] = 0
            for s in self.dsem[e]:
                self.semobj[id(s)] = s
        self.out_toks = []

    def _waits(self, eng, reads, writes, extra=()):
        need = {}

        def add(tok):
            if tok is None:
                return
            s, v, e = tok
            if e == eng and eng == "pe":
                return
            k = id(s)
            if need.get(k, 0) < v:
                need[k] = v

        for t in reads:
            add(t.w)
            if t.excl:
                for r in t.r:
                    add(r)
        for t in writes:
            add(t.w)
            for r in t.r:
                add(r)
        for tok in extra:
            add(tok)
        out = []
        for k, v in need.items():
            if self.seen[eng].get(k, 0) >= v:
                continue
            self.seen[eng][k] = v
            out.append((self.semobj[k], v))
        return out

    def _upd(self, tok, reads, writes):
        ex = [t for t in reads if t.excl]
        reads = [t for t in reads if not t.excl]
        writes = list(writes) + ex
        for t in reads:
            t.r.append(tok)
            if len(t.r) > 48:
                best = {}
                for (s, v, e) in t.r:
                    if best.get(id(s), (None, 0, None))[1] < v:
                        best[id(s)] = (s, v, e)
                t.r = list(best.values())
        for t in writes:
            t.w = tok
            t.r = []

    def _lim(self):
        self.nops = getattr(self, "nops", 0) + 1
        lim = getattr(self, "limit", None)
        return lim is not None and self.nops > lim

    def emit(self, eng, fn, reads=(), writes=()):
        if self._lim():
            return None
        waits = self._waits(eng, reads, writes)
        self.cnt[eng] += 1
        tok = (self.sem[eng], self.cnt[eng], eng)
        self.q[eng].append((waits, fn, self.sem[eng], 1))
        self._upd(tok, reads, writes)
        return tok

    def dma(self, eng, fn, reads=(), writes=(), is_out=False):
        if self._lim():
            return None
        i = self.dpos[eng]
        self.dpos[eng] = (i + 1) % len(self.dsem[eng])
        s = self.dsem[eng][i]
        prev = self.dval[eng][i]
        extra = [(s, prev, "dma")] if prev > 0 else []
        waits = self._waits(eng, reads, writes, extra)
        self.dval[eng][i] = prev + 16
        tok = (s, prev + 16, "dma")
        self.q[eng].append((waits, fn, s, 16))
        self._upd(tok, reads, writes)
        if is_out:
            self.out_toks.append(tok)
        return tok

    def barrier(self):
        for e in self.ENGS:
            waits = []
            cand = [(self.sem[e2], self.cnt[e2]) for e2 in self.ENGS if e2 != e]
            for q in self.dsem:
                cand += [(s, v) for s, v in zip(self.dsem[q], self.dval[q])]
            for s, v in cand:
                if v > 0 and self.seen[e].get(id(s), 0) < v:
                    self.seen[e][id(s)] = v
                    waits.append((s, v))
            if waits:
                self.q[e].append((waits, None, None, 0))

    def finish(self):
        need = {}
        for (s, v, e) in self.out_toks:
            if need.get(id(s), 0) < v:
                need[id(s)] = v
        waits = [(self.semobj[k], v) for k, v in need.items()]
        self.q["sp"].append((waits, None, None, 0))

    def build(self, block):
        h = {"sp": block.sync, "act": block.scalar, "dve": block.vector, "pool": block.gpsimd, "pe": block.tensor}

        def mk(e):
            def body(engh):
                for waits, fn, s, inc in self.q[e]:
                    for (ws, wv) in waits:
                        engh.wait_ge(ws, wv)
                    if fn is not None:
                        ins = fn(engh)
                        ins.then_inc(s, inc)
            return body

        for e in self.ENGS:
            h[e](mk(e))


class Arena:
    def __init__(self, base, nwords):
        self.base = base
        self.n = nwords
        self.off = 0
        self.hi = 0

    def reset(self):
        self.off = 0

    def alloc(self, shape, dt=F32):
        npart = shape[0]
        free = list(shape[1:])
        nel = int(np.prod(free))
        words = nel if dt != BF16 else (nel + 1) // 2
        assert self.off + words <= self.n, ("arena overflow", self.off, words, self.n)
        ap = self.base[:, self.off:self.off + words]
        self.off += words
        self.hi = max(self.hi, self.off)
        if dt == BF16:
            ap = ap.bitcast(BF16)[:, 0:nel]
        elif dt == I32:
            ap = ap.bitcast(I32)
        if len(free) == 2:
            ap = ap.rearrange("p (a b) -> p a b", a=free[0])
        elif len(free) == 3:
            ap = ap.rearrange("p (a b c) -> p a b c", a=free[0], b=free[1])
        if npart < 128:
            ap = ap[0:npart]
        return ap


def MM(out, lhsT, rhs, start=True, stop=True):
    return lambda e: e.matmul(out, lhsT=lhsT, rhs=rhs, start=start, stop=stop)


def MMS(items):
    def f(e):
        r = None
        for (out, lhsT, rhs, start, stop) in items:
            r = e.matmul(out, lhsT=lhsT, rhs=rhs, start=start, stop=stop)
        return r
    return f


def TRS(items):
    def f(e):
        r = None
        for (out, in_, ident) in items:
            r = e.transpose(out=out, in_=in_, identity=ident)
        return r
    return f


def ACT(out, in_, func, bias=None, scale=None, accum_out=None):
    kw = {}
    if bias is not None:
        kw["bias"] = bias
    if scale is not None:
        kw["scale"] = scale
    if accum_out is not None:
        kw["accum_out"] = accum_out
    return lambda e: e.activation(out=out, in_=in_, func=func, **kw)


def TT(out, in0, in1, op):
    return lambda e: e.tensor_tensor(out=out, in0=in0, in1=in1, op=op)


def TS(out, in0, s1, op0, s2=None, op1=None):
    if op1 is None:
        return lambda e: e.tensor_scalar(out=out, in0=in0, scalar1=s1, scalar2=None, op0=op0)
    return lambda e: e.tensor_scalar(out=out, in0=in0, scalar1=s1, scalar2=s2, op0=op0, op1=op1)


def STT(out, in0, scalar, in1, op0, op1):
    return lambda e: e.scalar_tensor_tensor(out=out, in0=in0, scalar=scalar, in1=in1, op0=op0, op1=op1)


def CP(out, in_):
    return lambda e: e.tensor_copy(out=out, in_=in_)


def MSET(ap, v):
    return lambda e: e.memset(ap, v)


def ASEL(out, in_, pattern, op, base, cm, fill=0.0):
    return lambda e: e.affine_select(out=out, in_=in_, pattern=pattern, compare_op=op, fill=fill, base=base, channel_multiplier=cm)


def DMA(out, in_):
    return lambda e: e.dma_start(out=out, in_=in_)


def build_nc(npool=2560, phases="A,S,PF,PG,O", sstage=9):
    PH = phases.split(",")
    nc = bass.Bass("TRN2", target_bir_lowering=False)

    def din(name, shape, dt=F32):
        return nc.dram_tensor(name, shape, dt, kind="ExternalInput").ap()

    def dout(name, shape, dt=F32):
        return nc.dram_tensor(name, shape, dt, kind="ExternalOutput").ap()

    xp = din("xp", [T, D])
    xs = din("xs", [NS, D])
    wblk = din("wblk", [32, 128, 8, 128])
    wsm = din("wsm", [128, 8, 96])
    nwl = din("nwl", [128, 8])
    fnw = din("fnw", [1, D])
    fbd = din("fb", [4, 1])
    woutl = din("woutl", [128, 8, D])
    convw = din("convw", [128, 12, 4])
    alog = din("alog", [4, 1])
    dtb = din("dtb", [4, 1])
    onw = din("onw", [128, 1])
    ck = din("ck", [npool * 128, 512])
    cv = din("cv", [npool * 128, 512])
    clf = din("clf", [npool, 512])
    ptab = din("ptab", [1, 256], I32)
    ptabc = din("ptabc", [128, 2], I32)
    ptabx = din("ptabx", [128, 16], I32)
    ssm_in = din("ssm_in", [16, 4, 128, 128])
    cst_in = din("cst_in", [128, 12, 16, 3])

    y_p = dout("y_p", [T, D])
    y_s = dout("y_s", [NS, D])
    kT_p = dout("kT_p", [4, 128, T])
    v_p = dout("v_p", [T, 512])
    lfT_p = dout("lfT_p", [4, T])
    ssm_p = dout("ssm_p", [4, 128, 128])
    cvT_p = dout("cvT_p", [128, 12, 3])
    kT_s = dout("kT_s", [4, 128, NS])
    vT_s = dout("vT_s", [4, 128, NS])
    lfT_s = dout("lfT_s", [4, NS])
    ssm_s = dout("ssm_s", [16, 4, 128, 128])
    cvT_s = dout("cvT_s", [128, 12, NS])

    es = ExitStack()
    with es:
        P = Prog(nc, es)
        import os as _os
        if _os.environ.get("KLIM"):
            P.limit = int(_os.environ["KLIM"])

        def sb(name, shape, dt=F32):
            return es.enter_context(nc.sbuf_tensor(name, shape, dt))

        ps = [es.enter_context(nc.psum_tensor("ps%d" % i, [128, 512], F32)) for i in range(8)]
        tps = [Tk(excl=True) for _ in range(8)]
        tsl = [[tps[b]] * 8 for b in range(8)]

        def psl(b, s, n=128, m=128):
            return ps[b][0:n, s * 128:s * 128 + m]

        def pslb(b, s, n=128, m=128):
            return ps[b][:].bitcast(BF16)[0:n, s * 128:s * 128 + m]

        ARW = 24576
        big = sb("arena", [128, ARW])
        AR = Arena(big[:, :], ARW)

        ones_f = sb("ones_f", [128, 128])
        zeros_f = sb("zeros_f", [128, 128])
        ones_b = sb("ones_b", [128, 128], BF16)
        ident_b = sb("ident_b", [128, 128], BF16)
        ident_f = sb("ident_f", [128, 128])
        maskneg = sb("maskneg", [128, 128], BF16)
        sel = sb("sel", [4, 4, 128], BF16)
        sel_f = sb("sel_f", [4, 4, 128])
        ML_f = sb("ML_f", [128, 128])
        MU_f = sb("MU_f", [128, 128])
        LAST_f = sb("LAST_f", [128, 128])
        BLK4 = sb("BLK4", [64, 64])
        ML4 = sb("ML4", [64, 64])
        MU4 = sb("MU4", [64, 64])
        LAST4 = sb("LAST4", [64, 64])
        negU4 = sb("negU4", [64, 64], BF16)
        bm = sb("bm", [64, 16])
        bmT = sb("bmT", [128, 16, 64])
        SL16 = sb("SL16", [128, 128])
        rst4 = sb("rst4", [4, 16, 4])
        nw_sb = sb("nw_sb", [128, 8])
        nfb_sb = sb("nfb_sb", [4, 1])
        dtb_sb = sb("dtb_sb", [4, 1])
        negA = sb("negA", [4, 1])
        onw_s = sb("onw_s", [128, 1])
        convw_sb = sb("convw_sb", [128, 12, 4])
        fnw_b = sb("fnw_b", [128, D])
        wsm_st = sb("wsm_st", [128, 8, 96])
        wsm_b = sb("wsm_b", [128, 8, 96], BF16)
        t_const = Tk()
        t_wsm = Tk()
        epsc = sb("epsc", [128, 4])

        def RSQ(out, in_, col, reads, writes):
            P.emit("act", ACT(out, in_, AF.Ln, bias=epsc[0:out.shape[0], col:col + 1]), reads=list(reads) + [t_const], writes=list(writes))
            P.emit("act", ACT(out, out, AF.Exp, scale=-0.5), reads=list(writes), writes=list(writes))
        GE = ALU.is_ge
        EQ = ALU.is_equal

        def pe_(fn):
            P.emit("pool", fn, reads=[t_const], writes=[t_const])

        pe_(MSET(ones_f[:], 1.0))
        pe_(MSET(epsc[:, 0:1], EPS))
        pe_(MSET(epsc[:, 1:2], D * EPS))
        pe_(MSET(epsc[:, 2:3], 128.0 * EPS))
        pe_(MSET(zeros_f[:], 0.0))
        pe_(MSET(ones_b[:], 1.0))
        pe_(ASEL(ident_b[:], ones_f[:], [[-1, 128]], EQ, 0, 1))
        pe_(ASEL(ident_f[:], ones_f[:], [[-1, 128]], EQ, 0, 1))
        pe_(MSET(maskneg[:], 0.0))
        pe_(ASEL(maskneg[:], maskneg[:], [[1, 128]], GE, 0, -1, fill=-10000.0))
        pe_(ASEL(sel[:], ones_f[0:4, :].unsqueeze(1).to_broadcast([4, 4, 128]), [[-1, 4], [0, 128]], EQ, 0, 1))
        pe_(ASEL(sel_f[:], ones_f[0:4, :].unsqueeze(1).to_broadcast([4, 4, 128]), [[-1, 4], [0, 128]], EQ, 0, 1))
        pe_(ASEL(ML_f[:], ones_f[:], [[-1, 128]], GE, -1, 1))
        pe_(ASEL(MU_f[:], ones_f[:], [[1, 128]], GE, 0, -1))
        pe_(ASEL(LAST_f[:], ones_f[:], [[0, 128]], EQ, -127, 1))
        v3 = lambda t: t[:].rearrange("p (s l) -> p s l", l=4)
        o3 = ones_f[0:64, 0:64].rearrange("p (s l) -> p s l", l=4)
        pe_(ASEL(v3(BLK4), o3, [[-4, 16], [0, 4]], GE, 0, 1))
        pe_(ASEL(v3(BLK4), v3(BLK4), [[4, 16], [0, 4]], GE, 3, -1))
        pe_(ASEL(v3(ML4), v3(BLK4), [[-4, 16], [-1, 4]], GE, -1, 1))
        pe_(ASEL(v3(MU4), v3(BLK4), [[4, 16], [1, 4]], GE, 0, -1))
        pe_(ASEL(v3(LAST4), o3, [[-4, 16], [0, 4]], EQ, -3, 1))
        pe_(TS(negU4[:], MU4[:], -1.0, ALU.add, 10000.0, ALU.mult))
        pe_(ASEL(bm[:], ones_f[0:64, 0:16], [[-4, 16]], GE, 0, 1))
        pe_(ASEL(bm[:], bm[:], [[4, 16]], GE, 3, -1))
        o16 = ones_f[:, 0:64].unsqueeze(1).to_broadcast([128, 16, 64])
        pe_(ASEL(bmT[:], o16, [[-4, 16], [1, 64]], GE, 0, 0))
        pe_(ASEL(bmT[:], bmT[:], [[4, 16], [-1, 64]], GE, 3, 0))
        s16 = SL16[:].rearrange("p (b l) -> p b l", l=16)
        pe_(ASEL(s16, ones_f[:].rearrange("p (b l) -> p b l", l=16), [[-16, 8], [0, 16]], GE, 0, 1))
        pe_(ASEL(s16, s16, [[16, 8], [0, 16]], GE, 15, -1))
        pe_(ASEL(s16, s16, [[16, 8], [1, 16]], GE, -1, -1))
        pe_(MSET(rst4[:], 1.0))
        pe_(MSET(rst4[:, :, 0:1], 0.0))

        P.dma("sp", DMA(nw_sb[:], nwl[:, :]), writes=[t_const])
        P.dma("sp", DMA(nfb_sb[:], fbd[:, :]), writes=[t_const])
        P.dma("sp", DMA(dtb_sb[:], dtb[:, :]), writes=[t_const])
        P.dma("sp", DMA(negA[:], alog[:, :]), writes=[t_const])
        P.dma("sp", DMA(onw_s[:], onw[:, :]), writes=[t_const])
        P.dma("sp", DMA(convw_sb[:], convw[:, :, :]), writes=[t_const])
        P.dma("sp", DMA(fnw_b[:], fnw.partition_broadcast(128)), writes=[t_const])
        P.dma("sp", DMA(wsm_st[:], wsm[:, :, :]), writes=[t_wsm])
        P.emit("dve", CP(wsm_b[:], wsm_st[:]), reads=[t_wsm], writes=[t_wsm])
        P.emit("dve", TS(nfb_sb[:], nfb_sb[:], -1.0, ALU.mult), reads=[t_const], writes=[t_const])
        P.emit("act", ACT(negA[:], negA[:], AF.Exp), reads=[t_const], writes=[t_const])
        P.emit("dve", TS(negA[:], negA[:], -1.0, ALU.mult), reads=[t_const], writes=[t_const])
        P.emit("dve", TS(onw_s[:], onw_s[:], float(128 ** 0.5), ALU.mult), reads=[t_const], writes=[t_const])
        P.emit("dve", TS(nw_sb[:], nw_sb[:], float(D ** 0.5), ALU.mult), reads=[t_const], writes=[t_const])
        P.emit("dve", TS(fnw_b[:], fnw_b[:], float(D ** 0.5), ALU.mult), reads=[t_const], writes=[t_const])

        hT = sb("hT", [128, 8, T], BF16)
        hTs = sb("hTs", [128, 8, NS], BF16)
        oT = sb("oT", [128, 8, T], BF16)
        oTs = sb("oTs", [128, 8, NS], BF16)
        t_hT = [Tk() for _ in range(16)]
        t_hTs = Tk()
        t_oT = [[Tk() for _ in range(16)] for _ in range(8)]
        t_oTs = [Tk() for _ in range(8)]
        sm_tok = sb("sm_tok", [128, 16, 96])
        t_smtok = Tk()
        cvst = sb("cvst", [128, 12, 3])
        t_cvst = Tk()

        NW = 3
        wst = [sb("wst%d" % i, [128, 8, 128]) for i in range(NW)]
        wbf = [sb("wbf%d" % i, [128, 8, 128], BF16) for i in range(NW)]
        t_wst = [Tk() for _ in range(NW)]
        t_wbf = [Tk() for _ in range(NW)]
        wctr = [0]

        def load_w(b):
            s = wctr[0] % NW
            wctr[0] += 1
            P.dma("sp" if wctr[0] % 2 else "pool", DMA(wst[s][:], wblk[b]), writes=[t_wst[s]])
            if wctr[0] % 2 == 0:
                P.emit("act", ACT(wbf[s][:], wst[s][:], AF.Copy), reads=[t_wst[s]], writes=[t_wbf[s]])
            else:
                P.emit("dve", CP(wbf[s][:], wst[s][:]), reads=[t_wst[s]], writes=[t_wbf[s]])
            return wbf[s], t_wbf[s]

        def fm_group(w, tw, g, bank):
            items = [(ps[bank][:, :], w[:, kc, :], hT[:, kc, g * 512:(g + 1) * 512], kc == 0, kc == 7) for kc in range(8)]
            P.emit("pe", MMS(items), reads=[tw] + t_hT[g * 4:(g + 1) * 4], writes=[tps[bank]])

        def tm_group(w, tw, g, bank, ncol=128):
            pv = ps[bank][:, 0:4 * ncol].rearrange("p (a b) -> p a b", a=4)
            items = []
            for j in range(4):
                i = g * 4 + j
                for kc in range(8):
                    items.append((pv[:, j, :], hT[:, kc, i * 128:(i + 1) * 128], w[:, kc, 0:ncol], kc == 0, kc == 7))
            P.emit("pe", MMS(items), reads=[tw] + t_hT[g * 4:(g + 1) * 4], writes=[tps[bank]])
            return pv

        def phase_a():
            xt = [AR.alloc([128, D]) for _ in range(2)]
            hn = [AR.alloc([128, D], BF16) for _ in range(2)]
            junk = AR.alloc([128, D], BF16)
            stat = [AR.alloc([128, 4]) for _ in range(2)]
            t_xt = [Tk() for _ in range(2)]
            t_hn = [Tk() for _ in range(2)]
            t_junk = Tk()
            t_stat = [Tk() for _ in range(2)]
            for i in range(17):
                n = 128 if i < 16 else NS
                s = i % 2
                src = xp[i * 128:(i + 1) * 128, :] if i < 16 else xs[:, :]
                P.dma("sp", DMA(xt[s][:n, :], src), writes=[t_xt[s]])
                P.emit("act", ACT(junk[:n, :], xt[s][:n, :], AF.Square, accum_out=stat[s][:n, 0:1]), reads=[t_xt[s]], writes=[t_junk, t_stat[s]])
                RSQ(stat[s][:n, 1:2], stat[s][:n, 0:1], 1, [t_stat[s]], [t_stat[s]])
                P.emit("act", ACT(hn[s][:n, :], xt[s][:n, :], AF.Copy, scale=stat[s][:n, 1:2]), reads=[t_xt[s], t_stat[s]], writes=[t_hn[s]])
                pv = ps[s][:].bitcast(BF16).rearrange("p (a b) -> p a b", a=8)
                items = [(pv[:, kc, :n], hn[s][:n, kc * 128:(kc + 1) * 128], ident_b[:n, :n]) for kc in range(8)]
                P.emit("pe", TRS(items), reads=[t_hn[s], t_const], writes=[tps[s]])
                if i < 16:
                    P.emit("dve", TT(hT[:, :, i * 128:(i + 1) * 128], pv, nw_sb[:, :].unsqueeze(2).to_broadcast([128, 8, 128]), ALU.mult), reads=[tps[s], t_const], writes=[t_hT[i]])
                else:
                    P.emit("dve", TT(hTs[:, :, :], pv[:, :, :NS], nw_sb[:, :].unsqueeze(2).to_broadcast([128, 8, NS]), ALU.mult), reads=[tps[s], t_const], writes=[t_hTs])

        INT_F = ["tabs", "E", "X"]
        INT_B = ["egr", "A", "Bm", "Xb", "kbg", "vb"]
        INT_B2 = ["QP0", "QP1"]
        OUT_F = ["u"]
        OUT_B = ["kdec", "wT", "qdT", "qkT"]

        def gdn_bufs(internal=True, outputs=True):
            B = {}
            if internal:
                for nm in INT_F:
                    B[nm] = AR.alloc([128, 128])
                for nm in INT_B:
                    B[nm] = AR.alloc([128, 128], BF16)
                for nm in INT_B2:
                    B[nm] = AR.alloc([128, 2, 128])
            if outputs:
                for nm in OUT_F:
                    B[nm] = AR.alloc([128, 128])
                for nm in OUT_B:
                    B[nm] = AR.alloc([128, 128], BF16)
            B["t"] = {k: Tk() for k in list(B.keys())}
            return B

        def merge_bufs(Bi, Bo):
            B = dict(Bi)
            B.update({k: v for k, v in Bo.items() if k != "t"})
            B["t"] = dict(Bi["t"])
            B["t"].update(Bo["t"])
            return B

        def gdn_pre(n, K, kTa, qTa, vTa, gcTc, h, gc_col, beta_col, bg_col, ekl_col, ML, MU, rd, B, bk=0):
            t = B["t"]
            tb = tps[bk]
            P.emit("pe", MM(psl(bk, 0, 128, n), sel_f[0:4, h, :], gcTc), reads=rd + [t_const], writes=[tb])
            yield
            P.emit("dve", TS(B["tabs"][0:n, 0:n], psl(bk, 0, n, n), gc_col, ALU.subtract), reads=[tb, t_const] + rd, writes=[t["tabs"]])
            P.emit("act", ACT(B["egr"][:, 0:n], psl(bk, 0, 128, n), AF.Exp), reads=[tb], writes=[t["egr"]])
            yield
            P.emit("dve", STT(B["tabs"][0:n, 0:n], B["tabs"][0:n, 0:n], -1.0, B["tabs"][0:n, 0:n], ALU.mult, ALU.max), reads=[t["tabs"]], writes=[t["tabs"]])
            P.emit("pe", MMS([(psl(bk, 1, n, n), kTa, kTa, True, True), (psl(bk, 2, n, n), kTa, qTa, True, True)]), reads=rd, writes=[tb])
            yield
            P.emit("act", ACT(B["E"][0:n, 0:n], B["tabs"][0:n, 0:n], AF.Exp, scale=-1.0), reads=[t["tabs"]], writes=[t["E"]])
            P.emit("dve", TT(B["qdT"][:, 0:n], qTa, B["egr"][:, 0:n], ALU.mult), reads=[t["egr"]] + rd, writes=[t["qdT"]])
            yield
            P.emit("dve", TT(B["tabs"][0:n, 0:n], B["E"][0:n, 0:n], ML[0:n, 0:n], ALU.mult), reads=[t["E"], t_const], writes=[t["tabs"]])
            P.emit("dve", TT(B["E"][0:n, 0:n], B["E"][0:n, 0:n], MU[0:n, 0:n], ALU.mult), reads=[t["tabs"], t_const], writes=[t["E"]])
            yield
            P.emit("dve", STT(B["A"][0:n, 0:n], psl(bk, 1, n, n), beta_col, B["tabs"][0:n, 0:n], ALU.mult, ALU.mult), reads=[tb, t["tabs"]] + rd, writes=[t["A"]])
            P.emit("dve", TT(B["qkT"][0:n, 0:n], psl(bk, 2, n, n), B["E"][0:n, 0:n], ALU.mult), reads=[tb, t["E"]], writes=[t["qkT"]])
            yield
            P.emit("pe", TRS([(pslb(bk, 6, n, n), B["A"][0:n, 0:n], ident_b[0:n, 0:n])]), reads=[t["A"], t_const], writes=[tb])
            P.emit("pe", MMS([(psl(bk, 0, n, 128), kTa, ident_b[:, :], True, True), (psl(bk, 1, n, 128), vTa, ident_b[:, :], True, True)]), reads=rd + [t_const], writes=[tb])
            yield
            P.emit("act", ACT(B["Bm"][0:n, 0:n], pslb(bk, 6, n, n), AF.Copy), reads=[tb], writes=[t["Bm"]])
            P.emit("act", ACT(B["kbg"][0:n, :], psl(bk, 0, n, 128), AF.Copy, scale=bg_col), reads=[tb] + rd, writes=[t["kbg"]])
            yield
            P.emit("dve", TS(B["kdec"][0:n, :], psl(bk, 0, n, 128), ekl_col, ALU.mult), reads=[tb] + rd, writes=[t["kdec"]])
            P.emit("act", ACT(B["vb"][0:n, :], psl(bk, 1, n, 128), AF.Copy, scale=beta_col), reads=[tb] + rd, writes=[t["vb"]])
            P.emit("dve", TT(B["X"][0:n, 0:n], ident_b[0:n, 0:n], B["Bm"][0:n, 0:n], ALU.subtract), reads=[t["Bm"], t_const], writes=[t["X"]])
            yield
            Qa, Pa = B["A"][0:n, 0:n], B["Bm"][0:n, 0:n]
            tQ, tP = t["A"], t["Bm"]
            QPn = "QP0"
            for k in range(1, K + 1):
                mm = [(psl(bk, 2, n, n), Pa, Qa, True, True)]
                if k < K:
                    mm.append((psl(bk, 3, n, n), Qa, Pa, True, True))
                P.emit("pe", MMS(mm), reads=[tQ, tP], writes=[tb])
                yield
                if k < K:
                    P.emit("act", ACT(B[QPn][0:n, :, 0:n], ps[bk][0:n, 256:512].rearrange("p (a b) -> p a b", a=2)[:, :, 0:n], AF.Copy), reads=[tb], writes=[t[QPn]])
                else:
                    P.emit("act", ACT(B[QPn][0:n, 0, 0:n], psl(bk, 2, n, n), AF.Copy), reads=[tb], writes=[t[QPn]])
                yield
                Qa, Pa = B[QPn][0:n, 0, 0:n], B[QPn][0:n, 1, 0:n]
                tQ = tP = t[QPn]
                P.emit("pe", MM(psl(bk, 0, n, n), Qa, B["X"][0:n, 0:n]), reads=[tQ, t["X"]], writes=[tb])
                yield
                P.emit("dve", TT(B["X"][0:n, 0:n], psl(bk, 0, n, n), B["X"][0:n, 0:n], ALU.add), reads=[tb], writes=[t["X"]])
                yield
                QPn = "QP1" if QPn == "QP0" else "QP0"
            P.emit("act", ACT(B["Xb"][0:n, 0:n], B["X"][0:n, 0:n], AF.Copy), reads=[t["X"]], writes=[t["Xb"]])
            yield
            X = "Xb"
            P.emit("pe", MMS([(psl(bk, 1, n, 128), B[X][0:n, 0:n], B["vb"][0:n, :], True, True), (psl(bk, 2, 128, n), B["kbg"][0:n, :], B[X][0:n, 0:n], True, True)]), reads=[t[X], t["vb"], t["kbg"]], writes=[tb])
            yield
            P.emit("dve", CP(B["u"][0:n, :], psl(bk, 1, n, 128)), reads=[tb], writes=[t["u"]])
            P.emit("act", ACT(B["wT"][:, 0:n], psl(bk, 2, 128, n), AF.Copy), reads=[tb], writes=[t["wT"]])
            yield

        def run_all(gens):
            gens = list(gens)
            while gens:
                nxt = []
                for g in gens:
                    try:
                        next(g)
                        nxt.append(g)
                    except StopIteration:
                        pass
                gens = nxt

        def phase_s():
            sraw = AR.alloc([128, 32, NS])
            t_sraw = [Tk() for _ in range(4)]
            vtok = AR.alloc([NS, 4, 128], BF16)
            t_vtok = Tk()
            smt = AR.alloc([NS, 96])
            t_smt = Tk()
            for b in range(32):
                w, tw = load_w(b)
                j = b % 8
                bank = (b // 8) % 2
                items = [(ps[bank][:, j * 64:(j + 1) * 64], w[:, kc, :], hTs[:, kc, :], kc == 0, kc == 7) for kc in range(8)]
                P.emit("pe", MMS(items), reads=[tw, t_hTs], writes=[tps[bank]])
                if 24 <= b < 28:
                    items = [(ps[2][0:NS, (b - 24) * 128:(b - 23) * 128], hTs[:, kc, :], w[:, kc, :], kc == 0, kc == 7) for kc in range(8)]
                    P.emit("pe", MMS(items), reads=[tw, t_hTs], writes=[tps[2]])
                if j == 7:
                    P.emit("act", ACT(sraw[:, b - 7:b + 1, :], ps[bank][:, :].rearrange("p (a b) -> p a b", a=8), AF.Copy), reads=[tps[bank]], writes=[t_sraw[b // 8]])
            P.emit("dve", CP(vtok[:, :, :], ps[2][0:NS, :].rearrange("p (a b) -> p a b", a=4)), reads=[tps[2]], writes=[t_vtok])
            items = [(ps[3][0:NS, 0:96], hTs[:, kc, :], wsm_b[:, kc, :], kc == 0, kc == 7) for kc in range(8)]
            P.emit("pe", MMS(items), reads=[t_wsm, t_hTs], writes=[tps[3]])
            P.emit("dve", CP(smt[:, :], ps[3][0:NS, 0:96]), reads=[tps[3]], writes=[t_smt])
            items = [(ps[4][0:4, 0:NS], wsm_b[:, kc, 0:4], hTs[:, kc, :], kc == 0, kc == 7) for kc in range(8)]
            items += [(ps[4][0:4, NS:2 * NS], wsm_b[:, kc, 64:68], hTs[:, kc, :], kc == 0, kc == 7) for kc in range(8)]
            P.emit("pe", MMS(items), reads=[t_wsm, t_hTs], writes=[tps[4]])
            rows = AR.alloc([4, 8, NS])
            t_rows = Tk()
            P.emit("act", ACT(rows[:, 0, :], ps[4][0:4, 0:NS], AF.Exp, scale=-1.0, bias=nfb_sb[:, 0:1]), reads=[tps[4], t_const], writes=[t_rows])
            P.emit("act", ACT(rows[:, 0, :], rows[:, 0, :], AF.Ln, bias=1.0), reads=[t_rows], writes=[t_rows])
            P.emit("dve", TS(rows[:, 1, :], rows[:, 0, :], -1.0, ALU.mult), reads=[t_rows], writes=[t_rows])
            P.dma("sp", DMA(lfT_s[:, :], rows[:, 1, :]), reads=[t_rows], is_out=True)
            P.emit("act", ACT(rows[:, 5, :], ps[4][0:4, NS:2 * NS], AF.Exp, bias=dtb_sb[:, 0:1]), reads=[tps[4], t_const], writes=[t_rows])
            P.emit("act", ACT(rows[:, 5, :], rows[:, 5, :], AF.Ln, bias=1.0), reads=[t_rows], writes=[t_rows])
            P.emit("dve", TS(rows[:, 2, :], rows[:, 5, :], negA[:, 0:1], ALU.mult), reads=[t_rows, t_const], writes=[t_rows])
            P.emit("dve", lambda e: e.tensor_tensor_scan(out=rows[:, 3, :], data0=rst4[:].rearrange("p s l -> p (s l)"), data1=rows[:, 2, :], initial=0.0, op0=ALU.mult, op1=ALU.add), reads=[t_rows, t_const], writes=[t_rows])
            P.dma("sp", DMA(kT_s.rearrange("h p n -> p h n"), sraw[:, 20:24, :]), reads=[t_sraw[2]], is_out=True)
            P.dma("sp", DMA(vT_s.rearrange("h p n -> p h n"), sraw[:, 24:28, :]), reads=[t_sraw[3]], is_out=True)
            P.dma("sp", DMA(cvT_s[:, :, :], sraw[:, 0:12, :]), reads=[t_sraw[0], t_sraw[1]], is_out=True)

            if sstage < 1:
                return
            mark = AR.off
            cst_sb = AR.alloc([128, 12, 16, 3])
            xx = AR.alloc([128, 12, 16, 7])
            t_xx = Tk()
            P.dma("sp", DMA(cst_sb[:, :, :, :], cst_in[:, :, :, :]), writes=[t_xx])
            P.emit("dve", CP(xx[:, :, :, 0:3], cst_sb[:, :, :, :]), reads=[t_xx], writes=[t_xx])
            P.emit("act", ACT(xx[:, :, :, 3:7], sraw[:, 0:12, :].rearrange("p b (s l) -> p b s l", l=4), AF.Copy), reads=[t_sraw[0], t_sraw[1], t_xx], writes=[t_xx])
            cvs = AR.alloc([128, 12, 16, 4])
            ctmp = AR.alloc([128, 12, 16, 4])
            t_cvs = Tk()
            t_ctmp = Tk()
            for j in range(4):
                wj = convw_sb[:, :, j:j + 1].unsqueeze(3).to_broadcast([128, 12, 16, 4])
                if j == 0:
                    P.emit("dve", TT(cvs[:, :, :, :], xx[:, :, :, 0:4], wj, ALU.mult), reads=[t_xx, t_const], writes=[t_cvs])
                else:
                    P.emit("dve", TT(ctmp[:, :, :, :], xx[:, :, :, j:j + 4], wj, ALU.mult), reads=[t_xx, t_const], writes=[t_ctmp])
                    P.emit("dve", TT(cvs[:, :, :, :], cvs[:, :, :, :], ctmp[:, :, :, :], ALU.add), reads=[t_ctmp], writes=[t_cvs])
            scs = AR.alloc([128, 12, NS])
            t_scs = Tk()
            P.emit("act", ACT(scs[:, :, :], cvs[:, :, :, :].rearrange("p b s l -> p b (s l)"), AF.Silu), reads=[t_cvs], writes=[t_scs])
            sq = AR.alloc([128, 8, NS], BF16)
            t_sq = Tk()
            P.emit("act", ACT(sq[:, :, :], scs[:, 0:8, :], AF.Square), reads=[t_scs], writes=[t_sq])
            P.emit("pe", MM(ps[5][:, :], ones_b[:, :], sq[:, :, :].rearrange("p a b -> p (a b)")), reads=[t_sq, t_const], writes=[tps[5]])
            rs = AR.alloc([128, 8, NS])
            t_rs = Tk()
            RSQ(rs[:, :, :], ps[5][:, :].rearrange("p (a b) -> p a b", a=8), 0, [tps[5]], [t_rs])
            qTs = AR.alloc([128, 4, NS], BF16)
            kTs = AR.alloc([128, 4, NS], BF16)
            vTs = AR.alloc([128, 4, NS], BF16)
            zsg = AR.alloc([128, 4, NS])
            t_qkv = Tk()
            P.emit("dve", STT(qTs[:, :, :], scs[:, 0:4, :], SCALE, rs[:, 0:4, :], ALU.mult, ALU.mult), reads=[t_scs, t_rs], writes=[t_qkv])
            P.emit("dve", TT(kTs[:, :, :], scs[:, 4:8, :], rs[:, 4:8, :], ALU.mult), reads=[t_scs, t_rs], writes=[t_qkv])
            P.emit("dve", CP(vTs[:, :, :], scs[:, 8:12, :]), reads=[t_scs], writes=[t_qkv])
            P.emit("act", ACT(zsg[:, :, :], sraw[:, 12:16, :], AF.Silu), reads=[t_sraw[1]], writes=[t_qkv])
            tsc = AR.alloc([NS, 8, 4])
            t_tsc = Tk()
            P.emit("pe", TRS([(ps[5][0:NS, 0:4], rows[0:4, 3, :], ident_f[0:4, 0:4])]), reads=[t_rows, t_const, t_rs], writes=[tps[5]])
            P.emit("dve", CP(tsc[:, 0, :], ps[5][0:NS, 0:4]), reads=[tps[5]], writes=[t_tsc])
            P.emit("pe", MM(ps[5][0:NS, 4:8], LAST4[:, :], tsc[:, 0, :]), reads=[t_tsc, t_const], writes=[tps[5]])
            P.emit("dve", TT(tsc[:, 5, :], ps[5][0:NS, 4:8], tsc[:, 0, :], ALU.subtract), reads=[tps[5], t_tsc], writes=[t_tsc])
            P.emit("act", ACT(tsc[:, 4, :], tsc[:, 5, :], AF.Exp), reads=[t_tsc], writes=[t_tsc])
            P.emit("act", ACT(tsc[:, 2, :], tsc[:, 0, :], AF.Exp), reads=[t_tsc], writes=[t_tsc])
            P.emit("act", ACT(tsc[:, 1, :], smt[:, 32:36], AF.Exp, scale=-1.0), reads=[t_smt, t_tsc], writes=[t_tsc])
            P.emit("dve", TS(tsc[:, 1, :], tsc[:, 1, :], 1.0, ALU.add), reads=[t_tsc], writes=[t_tsc])
            P.emit("dve", lambda e: e.reciprocal(out=tsc[:, 1, :], in_=tsc[:, 1, :]), reads=[t_tsc], writes=[t_tsc])
            P.emit("dve", TT(tsc[:, 3, :], tsc[:, 1, :], tsc[:, 2, :], ALU.mult), reads=[t_tsc], writes=[t_tsc])
            gl = AR.alloc([128, 4, 16])
            t_gl = Tk()
            items = [(ps[5][:, 16 + h * 16:32 + h * 16], sel_f[0:4, h, :], rows[0:4, 3, 3:NS:4], True, True) for h in range(4)]
            P.emit("pe", MMS(items), reads=[t_rows, t_const, t_tsc], writes=[tps[5]])
            P.emit("act", ACT(gl[:, :, :], ps[5][:, 16:80].rearrange("p (h s) -> p h s", h=4), AF.Exp), reads=[tps[5]], writes=[t_gl])

            P.barrier()
            Bh = [gdn_bufs() for _ in range(4)]
            wTz = AR.alloc([128, 16, NS], BF16)
            qdTz = AR.alloc([128, 16, NS], BF16)
            kdz = AR.alloc([NS, 16, 128], BF16)
            vnew = AR.alloc([NS, 128], BF16)
            osq = AR.alloc([128, NS], BF16)
            rso = AR.alloc([128, NS])
            on_ = AR.alloc([128, NS])
            t_z = Tk()
            t_vn = Tk()
            t_o = Tk()
            Sst = [AR.alloc([128, 4, 128]) for _ in range(2)]
            t_Sst = [Tk() for _ in range(2)]
            Sall = AR.alloc([128, 16, 4, 128], BF16)
            t_Sall = [Tk() for _ in range(16)]
            for s in range(16):
                b2 = s % 2
                P.dma("sp" if s % 2 else "pool", DMA(Sst[b2][:, :, :], ssm_in[s].rearrange("h d e -> d h e")), writes=[t_Sst[b2]])
                if s % 2 == 0:
                    P.emit("act", ACT(Sall[:, s, :, :], Sst[b2][:, :, :], AF.Copy), reads=[t_Sst[b2]], writes=[t_Sall[s]])
                else:
                    P.emit("dve", CP(Sall[:, s, :, :], Sst[b2][:, :, :]), reads=[t_Sst[b2]], writes=[t_Sall[s]])
            NSB = 4
            S1 = [AR.alloc([128, 128]) for _ in range(NSB)]
            So = [AR.alloc([128, 128]) for _ in range(NSB)]
            t_S1 = [Tk() for _ in range(NSB)]
            t_So = [Tk() for _ in range(NSB)]
            uctr = 0
            run_all([gdn_pre(NS, 1, kTs[:, h, :], qTs[:, h, :], vTs[:, h, :], rows[0:4, 3, :], h,
                             tsc[:, 0, h:h + 1], tsc[:, 1, h:h + 1], tsc[:, 3, h:h + 1], tsc[:, 4, h:h + 1],
                             ML4, MU4, [t_qkv, t_rows, t_tsc], Bh[h], bk=h) for h in range(4)])
            for h in range(4):
                B = Bh[h]
                t = B["t"]
                P.emit("dve", TT(wTz[:, :, :], bmT[:, :, :], B["wT"][:, 0:NS].unsqueeze(1).to_broadcast([128, 16, NS]), ALU.mult), reads=[t["wT"], t_const], writes=[t_z])
                P.emit("dve", TT(qdTz[:, :, :], bmT[:, :, :], B["qdT"][:, 0:NS].unsqueeze(1).to_broadcast([128, 16, NS]), ALU.mult), reads=[t["qdT"], t_const], writes=[t_z])
                P.emit("dve", TT(kdz[:, :, :], B["kdec"][0:NS, :].unsqueeze(1).to_broadcast([NS, 16, 128]), bm[:, :].unsqueeze(2).to_broadcast([NS, 16, 128]), ALU.mult), reads=[t["kdec"], t_const], writes=[t_z])
                items = [(psl(4, 0, NS, 128), wTz[:, s, :], Sall[:, s, h, :], s == 0, s == 15) for s in range(16)]
                P.emit("pe", MMS(items), reads=[t_z] + t_Sall, writes=[tsl[4][0]])
                P.emit("dve", TT(vnew[:, :], B["u"][0:NS, :], psl(4, 0, NS, 128), ALU.subtract), reads=[tsl[4][0], t["u"]], writes=[t_vn])
                items = [(psl(4, 1, 128, NS), Sall[:, s, h, :], qdTz[:, s, :], s == 0, False) for s in range(16)]
                items.append((psl(4, 1, 128, NS), vnew[:, :], B["qkT"][0:NS, 0:NS], False, True))
                P.emit("pe", MMS(items), reads=[t_z, t_vn, t["qkT"]] + t_Sall, writes=[tsl[4][1]])
                P.emit("act", ACT(osq[:, :], psl(4, 1, 128, NS), AF.Square), reads=[tsl[4][1]], writes=[t_o])
                P.emit("pe", MM(psl(4, 2, 128, NS), ones_b[:, :], osq[:, :]), reads=[t_o, t_const], writes=[tsl[4][2]])
                RSQ(rso[:, :], psl(4, 2, 128, NS), 2, [tsl[4][2]], [t_o])
                P.emit("dve", STT(on_[:, :], psl(4, 1, 128, NS), onw_s[:, 0:1], rso[:, :], ALU.mult, ALU.mult), reads=[tsl[4][1], t_o, t_const], writes=[t_o])
                P.emit("dve", TT(oTs[:, h, :], on_[:, :], zsg[:, h, :], ALU.mult), reads=[t_o, t_qkv], writes=[t_oTs[h]])
                for s in range(16):
                    b2 = uctr % NSB
                    pb = 6 + (uctr % 2)
                    uctr += 1
                    P.dma("pool", DMA(S1[b2][:, :], ssm_in[s, h]), writes=[t_S1[b2]])
                    P.emit("pe", MM(psl(pb, 0), kdz[:, s, :], vnew[:, :]), reads=[t_z, t_vn], writes=[tps[pb]])
                    P.emit("dve", STT(So[b2][:, :], S1[b2][:, :], gl[:, h, s:s + 1], psl(pb, 0), ALU.mult, ALU.add), reads=[t_S1[b2], tps[pb], t_gl], writes=[t_So[b2]])
                    P.dma("sp", DMA(ssm_s[s, h], So[b2][:, :]), reads=[t_So[b2]], is_out=True)

            if sstage < 2:
                return
            P.barrier()
            AR.off = mark
            qTn = AR.alloc([128, 4, NS], BF16)
            kTn = AR.alloc([128, 4, NS], BF16)
            zsf = AR.alloc([128, 4, NS])
            t_fx = Tk()
            P.emit("dve", TS(qTn[:, :, :], sraw[:, 16:20, :], SCALE, ALU.mult), reads=[t_sraw[2]], writes=[t_fx])
            P.emit("dve", CP(kTn[:, :, :], sraw[:, 20:24, :]), reads=[t_sraw[2]], writes=[t_fx])
            P.emit("act", ACT(zsf[:, :, :], sraw[:, 28:32, :], AF.Silu), reads=[t_sraw[3]], writes=[t_fx])
            ptx = AR.alloc([128, 16], I32)
            idxK = AR.alloc([128, 32], I32)
            idxL = AR.alloc([128, 16], I32)
            prow = AR.alloc([1, 128])
            pm = AR.alloc([128, 4])
            t_idx = Tk()
            P.dma("sp", DMA(ptx[:, :], ptabx[:, :]), writes=[t_idx])
            P.emit("pool", lambda e: e.iota(prow[:, :].rearrange("p (a b) -> p a b", b=8), pattern=[[0, 16], [1, 8]], base=0, channel_multiplier=0, allow_small_or_imprecise_dtypes=True), writes=[t_idx])
            P.emit("pe", MM(ps[0][:, 0:1], prow[0:1, :], ones_f[0:1, 0:1]), reads=[t_idx, t_const], writes=[tps[0]])
            P.emit("dve", CP(pm[:, 0:1], ps[0][:, 0:1]), reads=[tps[0]], writes=[t_idx])
            P.emit("dve", TS(pm[:, 1:2], pm[:, 0:1], 2.0, ALU.mult), reads=[t_idx], writes=[t_idx])
            P.emit("dve", TS(pm[:, 2:3], pm[:, 0:1], 2.0, ALU.mult, 1.0, ALU.add), reads=[t_idx], writes=[t_idx])
            idxKv = idxK[:, :].rearrange("p (s f) -> p s f", f=2)
            P.emit("dve", TS(idxKv[:, :, 0], ptx[:, :], 16.0, ALU.mult, pm[:, 1:2], ALU.add), reads=[t_idx], writes=[t_idx])
            P.emit("dve", TS(idxKv[:, :, 1], ptx[:, :], 16.0, ALU.mult, pm[:, 2:3], ALU.add), reads=[t_idx], writes=[t_idx])
            P.emit("dve", TS(idxL[:, :], ptx[:, :], 8.0, ALU.mult, pm[:, 0:1], ALU.add), reads=[t_idx], writes=[t_idx])
            lfg = AR.alloc([128, 16, 64])
            lfc = AR.alloc([128, 64, 16])
            rst16 = AR.alloc([128, 64, 16])
            tot = AR.alloc([128, 64])
            off = AR.alloc([128, 64])
            FL = AR.alloc([128, 64])
            SLT = AR.alloc([128, 128])
            t_lf = Tk()
            clf_v = clf.rearrange("(r k) d -> r (k d)", k=2) if False else clf
            for s in range(16):
                P.dma("pool", lambda e, s=s: e.indirect_dma_start(out=lfg[:, s, :], out_offset=None, in_=clf.rearrange("r (a b) -> (r a) b", a=8), in_offset=bass.IndirectOffsetOnAxis(ap=idxL[:, s:s + 1], axis=0)), reads=[t_idx], writes=[t_lf])
            P.emit("dve", MSET(rst16[:, :, :], 1.0), writes=[t_lf])
            P.emit("dve", MSET(rst16[:, :, 0:1], 0.0), writes=[t_lf])
            P.emit("pool", ASEL(SLT[:, :], ones_f[:, :], [[1, 128]], GE, -1, -1), reads=[t_const], writes=[t_lf])
            P.emit("dve", CP(lfc[:, :, :].rearrange("p (s h) l -> p s h l", h=4), lfg[:, :, :].rearrange("p s (l h) -> p s h l", h=4)), reads=[t_lf], writes=[t_lf])
            P.emit("dve", lambda e: e.tensor_tensor_scan(out=lfc[:, :, :].rearrange("p a b -> p (a b)"), data0=rst16[:, :, :].rearrange("p a b -> p (a b)"), data1=lfc[:, :, :].rearrange("p a b -> p (a b)"), initial=0.0, op0=ALU.mult, op1=ALU.add), reads=[t_lf], writes=[t_lf])
            P.emit("dve", CP(tot[:, :], lfc[:, :, 15]), reads=[t_lf], writes=[t_lf])
            P.emit("pe", MMS([(ps[0][:, 64:128], SLT[:, :], tot[:, :], True, True), (ps[0][:, 128:192], ones_f[:, :], tot[:, :], True, True)]), reads=[t_lf, t_const], writes=[tps[0]])
            P.emit("dve", CP(off[:, :], ps[0][:, 64:128]), reads=[tps[0]], writes=[t_lf])
            P.emit("dve", CP(FL[:, :], ps[0][:, 128:192]), reads=[tps[0]], writes=[t_lf])
            P.emit("dve", TT(lfc[:, :, :], lfc[:, :, :], off[:, :].unsqueeze(2).to_broadcast([128, 64, 16]), ALU.add), reads=[t_lf], writes=[t_lf])
            P.emit("dve", TS(lfc[:, :, :], lfc[:, :, :], -1.0, ALU.mult), reads=[t_lf], writes=[t_lf])
            NF4 = lfc[:, :, :].rearrange("p (s h) l -> p s h l", h=4)
            P.emit("dve", lambda e: e.tensor_tensor_scan(out=rows[:, 4, :], data0=rst4[:].rearrange("p s l -> p (s l)"), data1=rows[:, 1, :], initial=0.0, op0=ALU.mult, op1=ALU.add), reads=[t_rows, t_const], writes=[t_rows])
            cumB = AR.alloc([128, 4, NS])
            FqB = AR.alloc([128, 4, NS])
            NFq = AR.alloc([NS, 4])
            t_fq = Tk()
            items = [(ps[4][:, h * NS:(h + 1) * NS], sel_f[0:4, h, :], rows[0:4, 4, :], True, True) for h in range(4)]
            P.emit("pe", MMS(items), reads=[t_rows, t_const], writes=[tps[4]])
            P.emit("act", ACT(cumB[:, :, :], ps[4][:, 0:256].rearrange("p (h n) -> p h n", h=4), AF.Copy), reads=[tps[4]], writes=[t_fq])
            P.emit("dve", TT(FqB[:, :, :].rearrange("p h (s l) -> p h s l", l=4), cumB[:, :, :].rearrange("p h (s l) -> p h s l", l=4), FL[:, :].rearrange("p (s h) -> p h s", h=4).unsqueeze(3).to_broadcast([128, 4, 16, 4]), ALU.add), reads=[t_fq, t_lf], writes=[t_fq])
            P.emit("pe", TRS([(ps[5][0:NS, 0:4], rows[0:4, 4, :], ident_f[0:4, 0:4])]), reads=[t_rows, t_const], writes=[tps[5]])
            P.emit("dve", TS(NFq[:, :], ps[5][0:NS, 0:4], -1.0, ALU.mult), reads=[tps[5]], writes=[t_fq])
            On = AR.alloc([128, 4, NS])
            Dn = AR.alloc([128, 4, NS])
            tn = AR.alloc([NS, NS])
            PTn = AR.alloc([NS, NS], BF16)
            t_n = Tk()
            t_OD = Tk()
            for h in range(4):
                items = [(psl(6, 0, NS, NS), kTn[:, h, :], qTn[:, h, :], True, False), (psl(6, 0, NS, NS), ident_b[0:NS, 0:NS], negU4[:, :], False, True)]
                P.emit("pe", MMS(items), reads=[t_fx, t_const], writes=[tsl[6][0]])
                P.emit("dve", TT(tn[:, :], psl(6, 0, NS, NS), cumB[0:NS, h, :], ALU.add), reads=[tsl[6][0], t_fq], writes=[t_n])
                P.emit("act", ACT(PTn[:, :], tn[:, :], AF.Exp, bias=NFq[:, h:h + 1]), reads=[t_n, t_fq], writes=[t_n])
                items = [(psl(6, 1, 128, NS), vtok[:, h, :], PTn[:, :], True, True), (psl(6, 2, 128, NS), ones_b[0:NS, :], PTn[:, :], True, True)]
                P.emit("pe", MMS(items), reads=[t_n, t_vtok, t_const], writes=[tsl[6][1]])
                P.emit("act", ACT(On[:, h, :], psl(6, 1, 128, NS), AF.Copy), reads=[tsl[6][1]], writes=[t_OD])
                P.emit("dve", CP(Dn[:, h, :], psl(6, 2, 128, NS)), reads=[tsl[6][1]], writes=[t_OD])
            Kg = [AR.alloc([128, 8, 512], BF16) for _ in range(2)]
            Vg = [AR.alloc([128, 8, 512], BF16) for _ in range(2)]
            KT = [AR.alloc([128, 32, 128], BF16) for _ in range(2)]
            t1b = [AR.alloc([128, 128]) for _ in range(2)]
            t2b = [AR.alloc([128, 128]) for _ in range(2)]
            PT = [AR.alloc([128, 128], BF16) for _ in range(2)]
            OD = AR.alloc([128, 32, 32])
            t_Kg = [Tk() for _ in range(2)]
            t_Vg = [Tk() for _ in range(2)]
            t_KT = [Tk() for _ in range(2)]
            t_t1 = [Tk() for _ in range(2)]
            t_t2 = [Tk() for _ in range(2)]
            t_PT = [Tk() for _ in range(2)]
            for u in range(32):
                s, f = u // 2, u % 2
                b2 = u % 2
                col = s * 2 + f
                P.dma("pool", lambda e, col=col, b2=b2: e.indirect_dma_start(out=Kg[b2][:, :, :].rearrange("p a b -> p (a b)"), out_offset=None, in_=ck.rearrange("(r k) d -> r (k d)", k=8), in_offset=bass.IndirectOffsetOnAxis(ap=idxK[:, col:col + 1], axis=0)), reads=[t_idx], writes=[t_Kg[b2]])
                P.dma("pool", lambda e, col=col, b2=b2: e.indirect_dma_start(out=Vg[b2][:, :, :].rearrange("p a b -> p (a b)"), out_offset=None, in_=cv.rearrange("(r k) d -> r (k d)", k=8), in_offset=bass.IndirectOffsetOnAxis(ap=idxK[:, col:col + 1], axis=0)), reads=[t_idx], writes=[t_Vg[b2]])
                for r in range(2):
                    for bk in range(2):
                        items = []
                        for q8 in range(8):
                            jh = r * 16 + bk * 8 + q8
                            j, h = jh // 4, jh % 4
                            items.append((pslb(bk, q8), Kg[b2][:, j, h * 128:(h + 1) * 128], ident_b[:, :]))
                        P.emit("pe", TRS(items), reads=[t_Kg[b2], t_const], writes=[tps[bk]])
                        dst = KT[b2][:, r * 16 + bk * 8:r * 16 + bk * 8 + 8, :]
                        src = ps[bk][:].bitcast(BF16).rearrange("p (a b) -> p a b", a=8)
                        if bk == 0:
                            P.emit("act", ACT(dst, src, AF.Copy), reads=[tps[bk]], writes=[t_KT[b2]])
                        else:
                            P.emit("dve", CP(dst, src), reads=[tps[bk]], writes=[t_KT[b2]])
                sb_ = 2 + b2
                items = [(ps[sb_][:, jh * 4:jh * 4 + 4], KT[b2][:, jh, :], qTn[:, jh % 4, s * 4:s * 4 + 4], True, True) for jh in range(32)]
                P.emit("pe", MMS(items), reads=[t_KT[b2], t_fx], writes=[tps[sb_]])
                nfk_v = NF4[:, s, :, f * 8:(f + 1) * 8].rearrange("p h l -> p l h").unsqueeze(3).to_broadcast([128, 8, 4, 4])
                P.emit("dve", TT(t1b[b2][:, :].rearrange("p (j h q) -> p j h q", j=8, h=4), ps[sb_][:, 0:128].rearrange("p (j h q) -> p j h q", j=8, h=4), nfk_v, ALU.add), reads=[tps[sb_], t_lf], writes=[t_t1[b2]])
                fq_v = FqB[:, :, s * 4:s * 4 + 4].unsqueeze(1).to_broadcast([128, 8, 4, 4])
                P.emit("dve", TT(t2b[b2][:, :].rearrange("p (j h q) -> p j h q", j=8, h=4), t1b[b2][:, :].rearrange("p (j h q) -> p j h q", j=8, h=4), fq_v, ALU.add), reads=[t_t1[b2], t_fq], writes=[t_t2[b2]])
                P.emit("act", ACT(PT[b2][:, :], t2b[b2][:, :], AF.Exp), reads=[t_t2[b2]], writes=[t_PT[b2]])
                ob = 4 + b2
                items = []
                for h in range(4):
                    for j in range(8):
                        items.append((ps[ob][:, h * 4:h * 4 + 4], Vg[b2][:, j, h * 128:(h + 1) * 128], PT[b2][:, (j * 4 + h) * 4:(j * 4 + h) * 4 + 4], j == 0, j == 7))
                for j in range(8):
                    items.append((ps[ob][:, 16:32], ones_b[:, :], PT[b2][:, j * 16:(j + 1) * 16], j == 0, j == 7))
                P.emit("pe", MMS(items), reads=[t_Vg[b2], t_PT[b2], t_const], writes=[tps[ob]])
                P.emit("dve", CP(OD[:, u, :], ps[ob][:, 0:32]), reads=[tps[ob]], writes=[t_OD])
            ODv = OD[:, :, :].rearrange("p (s f) c -> p s f c", f=2)
            osum = AR.alloc([128, 16, 16])
            dsum = AR.alloc([128, 16, 16])
            P.emit("dve", TT(osum[:, :, :], ODv[:, :, 0, 0:16], ODv[:, :, 1, 0:16], ALU.add), reads=[t_OD], writes=[t_n])
            P.emit("dve", TT(dsum[:, :, :], ODv[:, :, 0, 16:32], ODv[:, :, 1, 16:32], ALU.add), reads=[t_OD], writes=[t_n])
            osv = osum[:, :, :].rearrange("p s (h q) -> p h s q", h=4)
            dsv = dsum[:, :, :].rearrange("p s (h q) -> p h s q", h=4)
            Onv = On[:, :, :].rearrange("p h (s q) -> p h s q", q=4)
            Dnv = Dn[:, :, :].rearrange("p h (s q) -> p h s q", q=4)
            P.emit("dve", TT(Onv, Onv, osv, ALU.add), reads=[t_n, t_OD], writes=[t_OD])
            P.emit("dve", TT(Dnv, Dnv, dsv, ALU.add), reads=[t_n, t_OD], writes=[t_OD])
            P.emit("dve", lambda e: e.reciprocal(out=Dn[:, :, :], in_=Dn[:, :, :]), reads=[t_OD], writes=[t_OD])
            P.emit("dve", TT(On[:, :, :], On[:, :, :], Dn[:, :, :], ALU.mult), reads=[t_OD], writes=[t_OD])
            P.emit("dve", TT(oTs[:, 4:8, :], On[:, :, :], zsf[:, :, :], ALU.mult), reads=[t_OD, t_fx], writes=t_oTs[4:8])

        def phase_pfox():
            qT = AR.alloc([128, T], BF16)
            kT = AR.alloc([128, T], BF16)
            zsT = AR.alloc([128, T], BF16)
            v_bf = AR.alloc([128, 16, 128], BF16)
            t_qT = [Tk() for _ in range(4)]
            t_kT = [Tk() for _ in range(4)]
            t_zsT = [Tk() for _ in range(4)]
            t_vbf = [Tk() for _ in range(4)]
            stg = [AR.alloc([128, 512]) for _ in range(2)]
            t_stg = [Tk() for _ in range(2)]
            stgc = [0]
            nlfT = AR.alloc([4, T])
            FT = AR.alloc([4, T])
            FbfT = AR.alloc([4, T], BF16)
            NFk = AR.alloc([128, 16, 4])
            ftmp = AR.alloc([4, 512])
            t_F = Tk()
            t_ftmp = Tk()
            bankc = [0]

            def nb():
                b = [0, 1, 2, 5, 6, 7][bankc[0] % 6]
                bankc[0] += 1
                return b

            for g in range(4):
                bank = nb()
                pv = tm_group(wsm_b, t_wsm, g, bank, ncol=96)
                P.emit("dve", CP(sm_tok[:, g * 4:(g + 1) * 4, :], pv), reads=[tps[bank]], writes=[t_smtok])
                bank = nb()
                items = [(ps[bank][0:4, :], wsm_b[:, kc, 0:4], hT[:, kc, g * 512:(g + 1) * 512], kc == 0, kc == 7) for kc in range(8)]
                P.emit("pe", MMS(items), reads=[t_wsm] + t_hT[g * 4:(g + 1) * 4], writes=[tps[bank]])
                P.emit("act", ACT(ftmp[:, :], ps[bank][0:4, :], AF.Exp, scale=-1.0, bias=nfb_sb[:, 0:1]), reads=[tps[bank], t_const], writes=[t_ftmp])
                P.emit("act", ACT(ftmp[:, :], ftmp[:, :], AF.Ln, bias=1.0), reads=[t_ftmp], writes=[t_ftmp])
                P.emit("dve", TS(nlfT[:, g * 512:(g + 1) * 512], ftmp[:, :], -1.0, ALU.mult), reads=[t_ftmp], writes=[t_F])
            P.dma("sp", DMA(lfT_p[:, :], nlfT[:, :]), reads=[t_F], is_out=True)
            P.emit("dve", lambda e: e.tensor_tensor_scan(out=FT[:, :], data0=ones_f[0:4, 0:1].to_broadcast([4, T]), data1=nlfT[:, :], initial=0.0, op0=ALU.mult, op1=ALU.add), reads=[t_F, t_const], writes=[t_F])
            P.emit("act", ACT(FbfT[:, :], FT[:, :], AF.Copy), reads=[t_F], writes=[t_F])
            bank = nb()
            pv = ps[bank][:, 0:64].rearrange("p (a b) -> p a b", a=16)
            P.emit("pe", TRS([(pv[:, kb, :], FT[0:4, kb * 128:(kb + 1) * 128], ident_f[0:4, 0:4]) for kb in range(16)]), reads=[t_F, t_const], writes=[tps[bank]])
            P.emit("dve", TS(NFk[:, :, :], pv, -1.0, ALU.mult), reads=[tps[bank]], writes=[t_F])

            NPT = 6
            SCB = [0, 1, 2, 5, 6, 7]
            pT = [AR.alloc([128, 512], BF16) for _ in range(NPT)]
            t_pT = [Tk() for _ in range(NPT)]
            rden = AR.alloc([128, 512])
            otmp = AR.alloc([128, 512])
            t_rden = Tk()
            t_otmp = Tk()

            for h in range(4):
                w, tw = load_w(16 + h)
                for g in range(4):
                    bank = nb()
                    fm_group(w, tw, g, bank)
                    P.emit("dve", TS(qT[:, g * 512:(g + 1) * 512], ps[bank][:, :], SCALE, ALU.mult), reads=[tps[bank]], writes=[t_qT[g]])
                w, tw = load_w(20 + h)
                for g in range(4):
                    bank = nb()
                    fm_group(w, tw, g, bank)
                    P.emit("act", ACT(kT[:, g * 512:(g + 1) * 512], ps[bank][:, :], AF.Copy), reads=[tps[bank]], writes=[t_kT[g]])
                    s = stgc[0] % 2
                    stgc[0] += 1
                    P.emit("dve", CP(stg[s][:, :], ps[bank][:, :]), reads=[tps[bank]], writes=[t_stg[s]])
                    P.dma("sp", DMA(kT_p[h, :, g * 512:(g + 1) * 512], stg[s][:, :]), reads=[t_stg[s]], is_out=True)
                w, tw = load_w(24 + h)
                for g in range(4):
                    bank = nb()
                    pv = tm_group(w, tw, g, bank)
                    P.emit("act", ACT(v_bf[:, g * 4:(g + 1) * 4, :], pv, AF.Copy), reads=[tps[bank]], writes=[t_vbf[g]])
                    s = stgc[0] % 2
                    stgc[0] += 1
                    P.emit("dve", CP(stg[s][:, :], ps[bank][:, :]), reads=[tps[bank]], writes=[t_stg[s]])
                    P.dma("sp", DMA(v_p[g * 512:(g + 1) * 512, h * 128:(h + 1) * 128].rearrange("(t p) d -> p t d", p=128), stg[s][:, :].rearrange("p (t d) -> p t d", t=4)), reads=[t_stg[s]], is_out=True)
                w, tw = load_w(28 + h)
                for g in range(4):
                    bank = nb()
                    fm_group(w, tw, g, bank)
                    P.emit("act", ACT(zsT[:, g * 512:(g + 1) * 512], ps[bank][:, :], AF.Silu), reads=[tps[bank]], writes=[t_zsT[g]])
                pctr = 0
                for qc in range(4):
                    nkb = 4 * qc + 4
                    items_ = []
                    for kb in range(nkb):
                        lo = max(qc * 512, kb * 128)
                        hi = (qc + 1) * 512
                        items_.append((kb, lo, hi, kb * 128 >= qc * 512))

                    def emit_s(it, slot, bank):
                        kb, lo, hi, diag = it
                        n = hi - lo
                        mm = [(ps[bank][:, 0:n], kT[:, kb * 128:(kb + 1) * 128], qT[:, lo:hi], True, False),
                              (ps[bank][:, 0:n], sel[0:4, h, :], FbfT[0:4, lo:hi], False, not diag)]
                        if diag:
                            mm.append((ps[bank][:, 0:128], ident_b[:, :], maskneg[:, :], False, True))
                        P.emit("pe", MMS(mm), reads=[t_kT[kb // 4], t_qT[qc], t_F, t_const], writes=[tps[bank]])
                        P.emit("act", ACT(pT[slot][:, 0:n], ps[bank][:, 0:n], AF.Exp, bias=NFk[:, kb, h:h + 1]), reads=[tps[bank], t_F], writes=[t_pT[slot]])

                    def emit_pv(it, slot, first, last):
                        kb, lo, hi, diag = it
                        n = hi - lo
                        o = lo - qc * 512
                        mm = [(ps[3][:, o:o + n], v_bf[:, kb, :], pT[slot][:, 0:n], first, last),
                              (ps[4][:, o:o + n], ones_b[:, :], pT[slot][:, 0:n], first, last)]
                        P.emit("pe", MMS(mm), reads=[t_vbf[kb // 4], t_pT[slot], t_const], writes=[tps[3], tps[4]])

                    LAG = 4
                    slots = []
                    for idx, it in enumerate(items_):
                        slot = pctr % NPT
                        bank = SCB[pctr % NPT]
                        pctr += 1
                        slots.append(slot)
                        emit_s(it, slot, bank)
                        if idx >= LAG:
                            jj = idx - LAG
                            emit_pv(items_[jj], slots[jj], jj == 0, False)
                    for jj in range(max(0, len(items_) - LAG), len(items_)):
                        emit_pv(items_[jj], slots[jj], jj == 0, jj == len(items_) - 1)
                    P.emit("dve", lambda e: e.reciprocal(out=rden[:, :], in_=ps[4][:, :]), reads=[tps[4]], writes=[t_rden])
                    P.emit("dve", TT(otmp[:, :], ps[3][:, :], rden[:, :], ALU.mult), reads=[tps[3], t_rden], writes=[t_otmp])
                    P.emit("dve", TT(oT[:, 4 + h, qc * 512:(qc + 1) * 512], otmp[:, :], zsT[:, qc * 512:(qc + 1) * 512], ALU.mult), reads=[t_otmp, t_zsT[qc]], writes=t_oT[4 + h][qc * 4:(qc + 1) * 4])

        def phase_pgdn():
            raw = AR.alloc([128, T + 3])
            cvb = AR.alloc([128, T])
            gT = raw[0:4, 3:T + 3]
            gtmp = cvb[0:4, 0:512]
            gcT = AR.alloc([4, T])
            t_g = Tk()
            t_gtmp = Tk()
            for g in range(4):
                bank = 6 + (g % 2)
                items = [(ps[bank][0:4, :], wsm_b[:, kc, 64:68], hT[:, kc, g * 512:(g + 1) * 512], kc == 0, kc == 7) for kc in range(8)]
                P.emit("pe", MMS(items), reads=[t_wsm] + t_hT[g * 4:(g + 1) * 4], writes=[tps[bank]])
                P.emit("act", ACT(gtmp[:, :], ps[bank][0:4, :], AF.Exp, bias=dtb_sb[:, 0:1]), reads=[tps[bank], t_const], writes=[t_gtmp])
                P.emit("act", ACT(gtmp[:, :], gtmp[:, :], AF.Ln, bias=1.0), reads=[t_gtmp], writes=[t_gtmp])
                P.emit("dve", TS(gT[:, g * 512:(g + 1) * 512], gtmp[:, :], negA[:, 0:1], ALU.mult), reads=[t_gtmp, t_const], writes=[t_g])
            for tl in range(16):
                P.emit("dve", lambda e, tl=tl: e.tensor_tensor_scan(out=gcT[:, tl * 128:(tl + 1) * 128], data0=ones_f[0:4, :], data1=gT[:, tl * 128:(tl + 1) * 128], initial=0.0, op0=ALU.mult, op1=ALU.add), reads=[t_g, t_const], writes=[t_g])
            tsc = AR.alloc([128, 6, 64])
            t_tsc = Tk()
            pv = ps[6][:, 0:64].rearrange("p (a b) -> p a b", a=16)
            P.emit("pe", TRS([(pv[:, tl, :], gcT[0:4, tl * 128:(tl + 1) * 128], ident_f[0:4, 0:4]) for tl in range(16)]), reads=[t_g, t_const], writes=[tps[6]])
            P.emit("dve", CP(tsc[:, 0, :], ps[6][:, 0:64]), reads=[tps[6]], writes=[t_tsc])
            P.emit("pe", MM(ps[6][:, 64:128], LAST_f[:, :], tsc[:, 0, :]), reads=[t_tsc, t_const], writes=[tps[6]])
            P.emit("dve", TT(tsc[:, 5, :], ps[6][:, 64:128], tsc[:, 0, :], ALU.subtract), reads=[tps[6], t_tsc], writes=[t_tsc])
            P.emit("act", ACT(tsc[:, 4, :], tsc[:, 5, :], AF.Exp), reads=[t_tsc], writes=[t_tsc])
            P.emit("act", ACT(tsc[:, 2, :], tsc[:, 0, :], AF.Exp), reads=[t_tsc], writes=[t_tsc])
            P.emit("act", ACT(tsc[:, 1, :].rearrange("p (t h) -> p t h", h=4), sm_tok[:, :, 32:36], AF.Exp, scale=-1.0), reads=[t_smtok], writes=[t_tsc])
            P.emit("dve", TS(tsc[:, 1, :], tsc[:, 1, :], 1.0, ALU.add), reads=[t_tsc], writes=[t_tsc])
            P.emit("dve", lambda e: e.reciprocal(out=tsc[:, 1, :], in_=tsc[:, 1, :]), reads=[t_tsc], writes=[t_tsc])
            P.emit("dve", TT(tsc[:, 3, :], tsc[:, 1, :], tsc[:, 2, :], ALU.mult), reads=[t_tsc], writes=[t_tsc])
            gl = AR.alloc([128, 4, 16])
            t_gl = Tk()
            items = [(ps[7][:, h * 16:(h + 1) * 16], sel_f[0:4, h, :], gcT[0:4, 127:T:128], True, True) for h in range(4)]
            P.emit("pe", MMS(items), reads=[t_g, t_const], writes=[tps[7]])
            P.emit("act", ACT(gl[:, :, :], ps[7][:, 0:64].rearrange("p (h s) -> p h s", h=4), AF.Exp), reads=[tps[7]], writes=[t_gl])

            sq = [AR.alloc([128, 512], BF16)] * 2
            rs = [AR.alloc([128, 512])] * 2
            t_sq = [Tk()] * 2
            t_rs = [Tk()] * 2
            t_raw = Tk()
            t_cvb = Tk()
            QS = []
            for _ in range(2):
                Q = {}
                for nm in ["q", "k", "v", "z"]:
                    Q[nm] = AR.alloc([128, T], BF16)
                    Q["t_" + nm] = Tk()
                QS.append(Q)
            S = AR.alloc([128, 128])
            Sb = AR.alloc([128, 128], BF16)
            vnew = AR.alloc([128, 128], BF16)
            osq = AR.alloc([128, 128], BF16)
            rso = AR.alloc([128, 128])
            on_ = AR.alloc([128, 128])
            t_S = Tk()
            t_Sb = Tk()
            t_vn = Tk()
            t_o = Tk()
            t_on = Tk()
            Bint = [gdn_bufs(True, False) for _ in range(4)]
            Bout = [[gdn_bufs(False, True) for _ in range(4)] for _ in range(2)]
            P.emit("dve", MSET(raw[:, 0:3], 0.0), reads=[t_g, t_gtmp], writes=[t_raw])
            sqc = [0]

            def stage_gen(h, st):
                Q = QS[st]
                for wh, nm in enumerate(["q", "k", "v"]):
                    dst, tdst = Q[nm], Q["t_" + nm]
                    b = wh * 4 + h
                    w, tw = load_w(b)
                    yield
                    for g in range(4):
                        bank = 6 + (g % 2)
                        fm_group(w, tw, g, bank)
                        P.emit("act", ACT(raw[:, 3 + g * 512:3 + (g + 1) * 512], ps[bank][:, :], AF.Copy), reads=[tps[bank], t_g, t_gtmp], writes=[t_raw])
                        yield
                    P.emit("dve", TS(cvb[:, :], raw[:, 0:T], convw_sb[:, b, 0:1], ALU.mult), reads=[t_raw, t_const, t_g, t_gtmp], writes=[t_cvb])
                    yield
                    for j in range(1, 4):
                        P.emit("dve", STT(cvb[:, :], raw[:, j:j + T], convw_sb[:, b, j:j + 1], cvb[:, :], ALU.mult, ALU.add), reads=[t_raw, t_const], writes=[t_cvb])
                        yield
                    P.emit("dve", CP(cvst[:, b, :], raw[:, T:T + 3]), reads=[t_raw], writes=[t_cvst])
                    P.emit("act", ACT(cvb[:, :], cvb[:, :], AF.Silu), reads=[t_cvb], writes=[t_cvb])
                    yield
                    if wh < 2:
                        for g in range(4):
                            j2 = sqc[0] % 2
                            sqc[0] += 1
                            bank = 6 + j2
                            P.emit("act", ACT(sq[j2][:, :], cvb[:, g * 512:(g + 1) * 512], AF.Square), reads=[t_cvb], writes=[t_sq[j2]])
                            yield
                            P.emit("pe", MM(ps[bank][:, :], ones_b[:, :], sq[j2][:, :]), reads=[t_sq[j2], t_const], writes=[tps[bank]])
                            yield
                            RSQ(rs[j2][:, :], ps[bank][:, :], 0, [tps[bank]], [t_rs[j2]])
                            yield
                            if wh == 0:
                                P.emit("dve", STT(dst[:, g * 512:(g + 1) * 512], cvb[:, g * 512:(g + 1) * 512], SCALE, rs[j2][:, :], ALU.mult, ALU.mult), reads=[t_cvb, t_rs[j2]], writes=[tdst])
                            else:
                                P.emit("dve", TT(dst[:, g * 512:(g + 1) * 512], cvb[:, g * 512:(g + 1) * 512], rs[j2][:, :], ALU.mult), reads=[t_cvb, t_rs[j2]], writes=[tdst])
                            yield
                    else:
                        P.emit("act", ACT(dst[:, :], cvb[:, :], AF.Copy), reads=[t_cvb], writes=[tdst])
                        yield
                w, tw = load_w(12 + h)
                yield
                for g in range(4):
                    bank = 6 + (g % 2)
                    fm_group(w, tw, g, bank)
                    P.emit("act", ACT(Q["z"][:, g * 512:(g + 1) * 512], ps[bank][:, :], AF.Silu), reads=[tps[bank]], writes=[Q["t_z"]])
                    yield

            def run_bg(gens, bg, steps):
                gens = list(gens)
                bgl = [bg] if bg is not None else []
                while gens:
                    nxt = []
                    for g in gens:
                        try:
                            next(g)
                            nxt.append(g)
                        except StopIteration:
                            pass
                    gens = nxt
                    for _ in range(steps):
                        if bgl:
                            try:
                                next(bgl[0])
                            except StopIteration:
                                bgl.pop()
                return bgl[0] if bgl else None

            run_all([stage_gen(0, 0)])
            for h in range(4):
                Q = QS[h % 2]
                kT, qT, vT, zsT = Q["k"], Q["q"], Q["v"], Q["z"]
                bg = stage_gen(h + 1, 1 - (h % 2)) if h < 3 else None
                P.emit("dve", MSET(S[:, :], 0.0), writes=[t_S])
                P.emit("dve", MSET(Sb[:, :], 0.0), writes=[t_Sb])

                def pre_gen(tl, B, bk, h=h, Q=Q, kT=kT, qT=qT, vT=vT):
                    c0, c1 = tl * 128, (tl + 1) * 128
                    ci = tl * 4 + h
                    return gdn_pre(128, 6, kT[:, c0:c1], qT[:, c0:c1], vT[:, c0:c1], gcT[0:4, c0:c1], h,
                                   tsc[:, 0, ci:ci + 1], tsc[:, 1, ci:ci + 1], tsc[:, 3, ci:ci + 1], tsc[:, 4, ci:ci + 1],
                                   ML_f, MU_f, [Q["t_q"], Q["t_k"], Q["t_v"], t_g, t_tsc], B, bk=bk)

                def seq_gen(tiles, Bs, h=h, Q=Q, zsT=zsT):
                    for tl, B in zip(tiles, Bs):
                        t = B["t"]
                        c0, c1 = tl * 128, (tl + 1) * 128
                        P.emit("pe", MM(psl(4, 0), B["wT"][:, :], Sb[:, :]), reads=[t["wT"], t_Sb], writes=[tps[4]])
                        yield
                        P.emit("dve", TT(vnew[:, :], B["u"][:, :], psl(4, 0), ALU.subtract), reads=[tps[4], t["u"]], writes=[t_vn])
                        yield
                        items = [(psl(5, 0), Sb[:, :], B["qdT"][:, :], True, False), (psl(5, 0), vnew[:, :], B["qkT"][:, :], False, True)]
                        P.emit("pe", MMS(items), reads=[t_Sb, t["qdT"], t_vn, t["qkT"]], writes=[tps[5]])
                        P.emit("pe", MM(psl(4, 2), B["kdec"][:, :], vnew[:, :]), reads=[t["kdec"], t_vn], writes=[tps[4]])
                        yield
                        P.emit("dve", STT(S[:, :], S[:, :], gl[:, h, tl:tl + 1], psl(4, 2), ALU.mult, ALU.add), reads=[tps[4], t_gl], writes=[t_S])
                        P.emit("act", ACT(osq[:, :], psl(5, 0), AF.Square), reads=[tps[5]], writes=[t_o])
                        yield
                        P.emit("act", ACT(Sb[:, :], S[:, :], AF.Copy), reads=[t_S], writes=[t_Sb])
                        P.emit("dve", TS(on_[:, :], psl(5, 0), onw_s[:, 0:1], ALU.mult), reads=[tps[5], t_const], writes=[t_on])
                        yield
                        P.emit("pe", MM(psl(5, 1), ones_b[:, :], osq[:, :]), reads=[t_o, t_const], writes=[tps[5]])
                        yield
                        RSQ(rso[:, :], psl(5, 1), 2, [tps[5]], [t_o])
                        yield
                        P.emit("dve", TT(on_[:, :], on_[:, :], rso[:, :], ALU.mult), reads=[t_o, t_on], writes=[t_on])
                        yield
                        P.emit("dve", TT(oT[:, h, c0:c1], on_[:, :], zsT[:, c0:c1], ALU.mult), reads=[t_on, Q["t_z"]], writes=[t_oT[h][tl]])
                        yield

                prev = None
                for grp in range(4):
                    tiles = [grp * 4 + i for i in range(4)]
                    Bs = [merge_bufs(Bint[i], Bout[grp % 2][i]) for i in range(4)]
                    gens = [pre_gen(tl, Bs[i], i) for i, tl in enumerate(tiles)]
                    if prev is not None:
                        gens.append(seq_gen(*prev))
                    bg = run_bg(gens, bg, 1)
                    prev = (tiles, Bs)
                bg = run_bg([seq_gen(*prev)], bg, 1)
                if bg is not None:
                    run_all([bg])
                P.dma("sp", DMA(ssm_p[h], S[:, :]), reads=[t_S], is_out=True)
            P.dma("sp", DMA(cvT_p[:, :, :], cvst[:, :, :]), reads=[t_cvst], is_out=True)

        def phase_out():
            wo_b = AR.alloc([128, 8, D], BF16)
            wo_st = [AR.alloc([128, D]) for _ in range(2)]
            t_wo = Tk()
            t_wost = [Tk() for _ in range(2)]
            for c in range(8):
                s = c % 2
                P.dma("sp" if c % 2 else "pool", DMA(wo_st[s][:, :], woutl[:, c, :]), writes=[t_wost[s]])
                if c % 2 == 0:
                    P.emit("act", ACT(wo_b[:, c, :], wo_st[s][:, :], AF.Copy), reads=[t_wost[s]], writes=[t_wo])
                else:
                    P.emit("dve", CP(wo_b[:, c, :], wo_st[s][:, :]), reads=[t_wost[s]], writes=[t_wo])
            NO = 4
            xt = [AR.alloc([128, D]) for _ in range(NO)]
            yt = [AR.alloc([128, D]) for _ in range(NO)]
            junk = AR.alloc([128, D], BF16)
            stat = [AR.alloc([128, 4]) for _ in range(NO)]
            t_xt = [Tk() for _ in range(NO)]
            t_yt = [Tk() for _ in range(NO)]
            t_junk = Tk()
            t_stat = [Tk() for _ in range(NO)]
            for i in range(17):
                n = 128 if i < 16 else NS
                s = i % NO
                src_ = xp[i * 128:(i + 1) * 128, :] if i < 16 else xs[:, :]
                P.dma("pool", DMA(xt[s][:n, :], src_), writes=[t_xt[s]])
                rd = [t_wo] + ([t_oT[c][i] for c in range(8)] if i < 16 else list(t_oTs))
                for hf in range(2):
                    bank = s * 2 + hf
                    items = []
                    for c in range(8):
                        lhs = oT[:, c, i * 128:(i + 1) * 128] if i < 16 else oTs[:, c, :]
                        items.append((ps[bank][:n, :], lhs, wo_b[:, c, hf * 512:(hf + 1) * 512], c == 0, c == 7))
                    P.emit("pe", MMS(items), reads=rd, writes=[tps[bank]])
                    P.emit("dve", TT(yt[s][:n, hf * 512:(hf + 1) * 512], ps[bank][:n, :], xt[s][:n, hf * 512:(hf + 1) * 512], ALU.add), reads=[tps[bank], t_xt[s]], writes=[t_yt[s]])
                P.emit("act", ACT(junk[:n, :], yt[s][:n, :], AF.Square, accum_out=stat[s][:n, 0:1]), reads=[t_yt[s]], writes=[t_junk, t_stat[s]])
                RSQ(stat[s][:n, 1:2], stat[s][:n, 0:1], 1, [t_stat[s]], [t_stat[s]])
                P.emit("act", ACT(yt[s][:n, :], yt[s][:n, :], AF.Copy, scale=stat[s][:n, 1:2]), reads=[t_yt[s], t_stat[s]], writes=[t_yt[s]])
                P.emit("dve", TT(yt[s][:n, :], yt[s][:n, :], fnw_b[:n, :], ALU.mult), reads=[t_yt[s], t_const], writes=[t_yt[s]])
                dst = y_p[i * 128:(i + 1) * 128, :] if i < 16 else y_s[:, :]
                P.dma("sp", DMA(dst, yt[s][:n, :]), reads=[t_yt[s]], is_out=True)

        for nm, fn in [("A", phase_a), ("S", phase_s), ("PF", phase_pfox), ("PG", phase_pgdn), ("O", phase_out)]:
            if nm in PH:
                fn()
                P.barrier()
                AR.reset()

        P.finish()
        with nc.Block() as block:
            P.build(block)
    return nc


def _prep_inputs(inp):
    f32 = np.float32
    ncores = int(np.asarray(inp["x_prompt"]).shape[0])
    npool = int(np.asarray(inp["cache_fox_k"]).shape[1])
    w_in = np.asarray(inp["w_in"][0], dtype=f32)
    cols = [i * 128 for i in range(16)] + [2056 + i * 128 for i in range(16)]
    wblk = np.stack([w_in[:, c:c + 128].reshape(8, 128, 128).transpose(1, 0, 2) for c in cols]).astype(f32)
    wsm = np.zeros((1024, 96), f32)
    wsm[:, 0:4] = w_in[:, 4104:4108]
    wsm[:, 32:36] = w_in[:, 2048:2052]
    wsm[:, 64:68] = w_in[:, 2052:2056]
    wsm = np.ascontiguousarray(wsm.reshape(8, 128, 96).transpose(1, 0, 2))
    nwl = np.ascontiguousarray(np.asarray(inp["norm_w"][0], dtype=f32).reshape(8, 128).T)
    fnw = np.asarray(inp["final_norm_w"], dtype=f32).reshape(1, 1024)
    fb = np.asarray(inp["fox_f_bias"][0], dtype=f32).reshape(4, 1)
    woutl = np.ascontiguousarray(np.asarray(inp["w_out"][0], dtype=f32).reshape(8, 128, 1024).transpose(1, 0, 2))
    convw = np.ascontiguousarray(np.asarray(inp["gdn_conv_w"][0], dtype=f32).reshape(4, 12, 128).transpose(2, 1, 0))
    common = dict(wblk=wblk, wsm=wsm, nwl=nwl, fnw=fnw, fb=fb, woutl=woutl, convw=convw,
                  alog=np.asarray(inp["gdn_a_log"], dtype=f32).reshape(4, 1),
                  dtb=np.asarray(inp["gdn_dt_bias"], dtype=f32).reshape(4, 1),
                  onw=np.asarray(inp["gdn_out_norm_w"], dtype=f32).reshape(128, 1),
                  ck=np.asarray(inp["cache_fox_k"]).reshape(npool * 128, 512),
                  cv=np.asarray(inp["cache_fox_v"]).reshape(npool * 128, 512),
                  clf=np.asarray(inp["cache_fox_logf"]).reshape(npool, 512))
    maps = []
    for c in range(ncores):
        m = dict(common)
        m["xp"] = np.ascontiguousarray(inp["x_prompt"][c])
        m["xs"] = np.ascontiguousarray(np.asarray(inp["x_sample"][c * 16:(c + 1) * 16]).reshape(64, 1024))
        pt = np.asarray(inp["page_table"][c * 16:(c + 1) * 16]).astype(np.int32).reshape(256)
        m["ptab"] = np.ascontiguousarray(pt.reshape(1, 256))
        m["ptabc"] = np.ascontiguousarray(pt.reshape(2, 128).T)
        m["ptabx"] = np.ascontiguousarray(np.repeat(pt.reshape(16, 16).T, 8, axis=0))
        m["ssm_in"] = np.ascontiguousarray(inp["state_gdn_ssm"][0, c * 16:(c + 1) * 16])
        cs = np.asarray(inp["state_gdn_conv"][0, c * 16:(c + 1) * 16])
        m["cst_in"] = np.ascontiguousarray(cs.reshape(16, 3, 12, 128).transpose(3, 2, 0, 1))
        maps.append(m)
    return maps, ncores, npool


_NC_CACHE = {}


def kernel(**inp):
    maps, ncores, npool = _prep_inputs(inp)
    import os
    if npool not in _NC_CACHE:
        _NC_CACHE[npool] = build_nc(npool, os.environ.get("KPH", "A,S,PF,PG,O"), int(os.environ.get("KSS", "9")))
    nc = _NC_CACHE[npool]
    res = run_bass_kernel_spmd(nc, maps, core_ids=list(range(ncores)))
    R = res.results
    f32 = np.float32
    rng = range(ncores)
    y_prompt = np.stack([R[c]["y_p"] for c in rng]).astype(f32)
    y_sample = np.concatenate([R[c]["y_s"].reshape(16, 4, 1024) for c in rng]).astype(f32)
    k_prompt = np.stack([R[c]["kT_p"].transpose(2, 0, 1) for c in rng])[None].astype(f32)
    v_prompt = np.stack([R[c]["v_p"].reshape(T, 4, 128) for c in rng])[None].astype(f32)
    logf_prompt = np.stack([R[c]["lfT_p"].T for c in rng])[None].astype(f32)
    ssm_prompt = np.stack([R[c]["ssm_p"] for c in rng])[None].astype(f32)
    conv_prompt = np.stack([R[c]["cvT_p"].transpose(2, 1, 0).reshape(3, 1536) for c in rng])[None].astype(f32)
    k_sample = np.concatenate([R[c]["kT_s"].transpose(2, 0, 1).reshape(16, 4, 4, 128) for c in rng])[None].astype(f32)
    v_sample = np.concatenate([R[c]["vT_s"].transpose(2, 0, 1).reshape(16, 4, 4, 128) for c in rng])[None].astype(f32)
    logf_sample = np.concatenate([R[c]["lfT_s"].T.reshape(16, 4, 4) for c in rng])[None].astype(f32)
    ssm_sample = np.concatenate([R[c]["ssm_s"] for c in rng])[None].astype(f32)
    conv_sample = np.concatenate([R[c]["cvT_s"].transpose(2, 1, 0).reshape(16, 4, 1536)[:, 1:4] for c in rng])[None].astype(f32)
    return (y_prompt, y_sample, k_prompt, v_prompt, logf_prompt, ssm_prompt, conv_prompt,
            k_sample, v_sample, logf_sample, ssm_sample, conv_sample)
```
